# Optimizing a Trainium2 kernel written in Bass

```python
import jax, jax.numpy as jnp
from jax import lax
import numpy as np

D_MODEL = 2048
BATCH = 4
SEQ = 2048
DEPTH = 2

N_MIXERS = 2
N_META = 16
D_FF = 5632
EPS = 1e-6
GLA_HEADS = 4
GLA_DK = D_MODEL // 2
GLA_DV = D_MODEL
GLA_HEAD_K = GLA_DK // GLA_HEADS
GLA_HEAD_V = GLA_DV // GLA_HEADS
GLA_GATE_RANK = 16
GLA_GATE_NORM = 16.0
GLA_CHUNK = 64
GLA_IN_W = GLA_DK + GLA_DK + GLA_DV + GLA_GATE_RANK + GLA_DV
POOL_WINDOWS = (2, 4, 8, 16)
POOL_GROUPS = 4
POOL_GROUP_W = D_MODEL // POOL_GROUPS
N_GLA_LAYERS = (DEPTH + 1) // 2
N_POOL_LAYERS = DEPTH // 2

kernel_name = 'hybrid_gla_pool_macaron'


def rms_norm(x, g):
    xf = x.astype(jnp.float32)
    y = xf * lax.rsqrt(jnp.mean(xf * xf, axis=-1, keepdims=True) + EPS)
    return (y * g.astype(jnp.float32)).astype(x.dtype)


def ffn_half(x, g, w_gate, w_up, w_down):
    h = rms_norm(x, g)
    return x + 0.5 * ((jax.nn.silu(h @ w_gate) * (h @ w_up)) @ w_down)


def gla_chunked(q, k, v, lg):
    B, H, T, dk = q.shape
    dv = v.shape[-1]
    C = GLA_CHUNK
    n = T // C

    def to_chunks(a):
        return jnp.moveaxis(a.reshape(B, H, n, C, a.shape[-1]), 2, 0)

    causal = jnp.tril(jnp.ones((C, C), dtype=bool))

    def step(S, inp):
        qc, kc, vc, gc = inp
        b = jnp.cumsum(gc, axis=2)
        b_last = b[:, :, -1:, :]
        o_inter = jnp.einsum('bhik,bhkv->bhiv', qc * jnp.exp(b), S)
        diff = b[:, :, :, None, :] - b[:, :, None, :, :]
        decay = jnp.exp(jnp.where(causal[:, :, None], diff, -jnp.inf))
        A = jnp.einsum('bhijk,bhjk->bhij', qc[:, :, :, None, :] * decay, kc)
        o = o_inter + jnp.einsum('bhij,bhjv->bhiv', A, vc)
        S = jnp.exp(b_last[:, :, 0, :])[..., None] * S + jnp.einsum('bhjk,bhjv->bhkv', kc * jnp.exp(b_last - b), vc)
        return S, o

    S0 = jnp.zeros((B, H, dk, dv), jnp.float32)
    _, o = lax.scan(step, S0, (to_chunks(q), to_chunks(k), to_chunks(v), to_chunks(lg)))
    return jnp.moveaxis(o, 0, 2).reshape(B, H, T, dv)


def gla_mixer(h, w_in, w_lr, b_lr, head_norm, w_out):
    B, L, _ = h.shape
    proj = h @ w_in
    q, k, v, lr, r = jnp.split(proj, [GLA_DK, 2 * GLA_DK, 2 * GLA_DK + GLA_DV, 2 * GLA_DK + GLA_DV + GLA_GATE_RANK], axis=-1)
    lg = jax.nn.log_sigmoid((lr @ w_lr + b_lr).astype(jnp.float32)) / GLA_GATE_NORM

    def heads(a, d):
        return a.reshape(B, L, GLA_HEADS, d).transpose(0, 2, 1, 3).astype(jnp.float32)

    q = heads(q, GLA_HEAD_K) * (GLA_HEAD_K ** -0.5)
    k = heads(k, GLA_HEAD_K)
    v = heads(v, GLA_HEAD_V)
    lg = heads(lg, GLA_HEAD_K)
    pad = (-N_META) % GLA_CHUNK
    padf = lambda a: jnp.pad(a, ((0, 0), (0, 0), (pad, 0), (0, 0)))
    o = gla_chunked(padf(q), padf(k), padf(v), padf(lg))[:, :, pad:, :]
    o = o * lax.rsqrt(jnp.mean(o * o, axis=-1, keepdims=True) + EPS) * head_norm.astype(jnp.float32)
    o = o.transpose(0, 2, 1, 3).reshape(B, L, GLA_DV).astype(h.dtype)
    return (o * jax.nn.silu(r)) @ w_out


def pool_mixer(h, w, b, scale):
    B, L, D = h.shape
    hf = h.astype(jnp.float32).reshape(B, L, POOL_GROUPS, POOL_GROUP_W)
    cs = jnp.cumsum(hf, axis=1)
    t = jnp.arange(L)
    outs = []
    for g, win in enumerate(POOL_WINDOWS):
        csg = cs[:, :, g]
        prev = jnp.pad(csg, ((0, 0), (win, 0), (0, 0)))[:, :L]
        cnt = jnp.minimum(t + 1, win).astype(jnp.float32)[:, None]
        outs.append((csg - prev) / cnt - hf[:, :, g])
    pooled = jnp.stack(outs, axis=2).astype(h.dtype)
    y = jnp.einsum('blgc,gcd->blgd', pooled, w) + b
    return y.reshape(B, L, D) * scale


def setup_inputs(seed: int = 0) -> dict:
    key = jax.random.key(seed)
    ks = jax.random.split(key, 20)
    f32 = jnp.float32
    nrm = lambda k, s, sc: jax.random.normal(k, s, f32) * sc
    return {
        'x': nrm(ks[0], (BATCH, SEQ, D_MODEL), 1.0),
        'meta': nrm(ks[1], (N_META, D_MODEL), 1.0),
        'ffn_norm': 1.0 + nrm(ks[2], (DEPTH, 2, D_MODEL), 0.02),
        'ffn_w_gate': nrm(ks[3], (DEPTH, 2, D_MODEL, D_FF), D_MODEL ** -0.5),
        'ffn_w_up': nrm(ks[4], (DEPTH, 2, D_MODEL, D_FF), D_MODEL ** -0.5),
        'ffn_w_down': nrm(ks[5], (DEPTH, 2, D_FF, D_MODEL), D_FF ** -0.5),
        'gla_norm': 1.0 + nrm(ks[6], (N_GLA_LAYERS, D_MODEL), 0.02),
        'gla_w_in': nrm(ks[7], (N_GLA_LAYERS, D_MODEL, GLA_IN_W), D_MODEL ** -0.5),
        'gla_w_lr': nrm(ks[8], (N_GLA_LAYERS, GLA_GATE_RANK, GLA_DK), GLA_GATE_RANK ** -0.5),
        'gla_b_lr': nrm(ks[9], (N_GLA_LAYERS, GLA_DK), 0.01),
        'gla_head_norm': 1.0 + nrm(ks[10], (N_GLA_LAYERS, GLA_HEAD_V), 0.02),
        'gla_w_out': nrm(ks[11], (N_GLA_LAYERS, GLA_DV, D_MODEL), GLA_DV ** -0.5),
        'pool_norm': 1.0 + nrm(ks[12], (N_POOL_LAYERS, D_MODEL), 0.02),
        'pool_w': nrm(ks[13], (N_POOL_LAYERS, POOL_GROUPS, POOL_GROUP_W, POOL_GROUP_W), POOL_GROUP_W ** -0.5),
        'pool_b': nrm(ks[14], (N_POOL_LAYERS, POOL_GROUPS, POOL_GROUP_W), 0.01),
        'pool_scale': 1.0 + nrm(ks[15], (N_POOL_LAYERS, D_MODEL), 0.02),
        'final_norm': 1.0 + nrm(ks[16], (D_MODEL,), 0.02),
    }


def reference(x, meta, ffn_norm, ffn_w_gate, ffn_w_up, ffn_w_down, gla_norm, gla_w_in, gla_w_lr, gla_b_lr,
              gla_head_norm, gla_w_out, pool_norm, pool_w, pool_b, pool_scale, final_norm):
    B = x.shape[0]
    m = jnp.broadcast_to(meta.astype(x.dtype)[None], (B, N_META, D_MODEL))
    x = jnp.concatenate([m, x], axis=1)
    for i in range(DEPTH):
        x = ffn_half(x, ffn_norm[i, 0], ffn_w_gate[i, 0], ffn_w_up[i, 0], ffn_w_down[i, 0])
        j = i // N_MIXERS
        if i % N_MIXERS == 0:
            x = x + gla_mixer(rms_norm(x, gla_norm[j]), gla_w_in[j], gla_w_lr[j], gla_b_lr[j], gla_head_norm[j], gla_w_out[j])
        else:
            x = x + pool_mixer(rms_norm(x, pool_norm[j]), pool_w[j], pool_b[j], pool_scale[j])
        x = ffn_half(x, ffn_norm[i, 1], ffn_w_gate[i, 1], ffn_w_up[i, 1], ffn_w_down[i, 1])
    return rms_norm(x, final_norm)[:, N_META:]
```

```python
import contextlib
import numpy as np
import concourse.bass as bass
import concourse.mybir as mybir
from concourse.bass_utils import run_bass_kernel_spmd

F32 = mybir.dt.float32
BF16 = mybir.dt.bfloat16
AF = mybir.ActivationFunctionType
ALU = mybir.AluOpType
AX = mybir.AxisListType

D = 2048
KC = 16
FF = 5632
T = 1040
TT = [(0, 512), (512, 512), (1024, 16)]
NCH = 9
NS = 5
SLAB = 4096
EPS = 1e-6
WINS = (2, 4, 8, 16)

ENGS = ("pe", "act", "dve", "pool", "sp")
BLOCK_ATTR = {"pe": "tensor", "act": "scalar", "dve": "vector", "pool": "gpsimd", "sp": "sync"}


class Prog:
    def __init__(self, nc):
        self.nc = nc
        self.ops = {e: [] for e in ENGS}
        self.lastw = {}
        self.rd = {}
        self.dcount = {}
        self.stack = contextlib.ExitStack()
        self.dsems = {}
        self.ucur = None
        self.uops = ({}, set())
        self.ubar = ({}, set())

    def sb(self, name, shape, dt):
        return self.stack.enter_context(self.nc.sbuf_tensor(name, list(shape), dt))

    def ps(self, name, shape, dt):
        return self.stack.enter_context(self.nc.psum_tensor(name, list(shape), dt))

    def add(self, eng, fn, r=(), w=(), dsem=None, ucls=None):
        idx = len(self.ops[eng])
        me = (eng, idx)
        deps = {}
        ddeps = set()

        def need(x):
            if x is None or x == me:
                return
            e, i = x
            o = self.ops[e][i]
            if o["dsem"] is not None:
                ddeps.add((o["dsem"], o["dcnt"]))
            else:
                if e == "pe" and eng == "pe":
                    return
                if deps.get(e, -1) < i:
                    deps[e] = i

        for k in r:
            need(self.lastw.get(k))
        for k in w:
            need(self.lastw.get(k))
            for x in self.rd.get(k, ()):
                need(x)
        if ucls is not None:
            if ucls != self.ucur:
                self.ubar = (dict(self.uops[0]), set(self.uops[1]))
                self.uops = ({}, set())
                self.ucur = ucls
            for e, i in self.ubar[0].items():
                if not (e == "pe" and eng == "pe") and deps.get(e, -1) < i:
                    deps[e] = i
            ddeps |= self.ubar[1]
        dcnt = None
        if dsem is not None:
            dcnt = self.dcount.get(dsem, 0) + 1
            self.dcount[dsem] = dcnt
        self.ops[eng].append(dict(fn=fn, deps=deps, ddeps=ddeps, dsem=dsem, dcnt=dcnt, sig=False))
        if ucls is not None:
            if dsem is not None:
                self.uops[1].add((dsem, dcnt))
            else:
                self.uops[0][eng] = idx
        for k in r:
            self.rd.setdefault(k, []).append(me)
        for k in w:
            self.lastw[k] = me
            self.rd[k] = []
        return me

    def emit(self):
        nc = self.nc
        for e in ENGS:
            for o in self.ops[e]:
                for (de, di) in o["deps"].items():
                    self.ops[de][di]["sig"] = True
        cnt = {}
        for e in ENGS:
            c = 0
            arr = []
            for o in self.ops[e]:
                if o["sig"]:
                    c += 1
                arr.append(c)
            cnt[e] = arr
        esem = {e: self.stack.enter_context(nc.semaphore("es_" + e)) for e in ENGS}
        for name in self.dcount:
            self.dsems[name] = self.stack.enter_context(nc.semaphore("ds_" + name))
        block = self.stack.enter_context(nc.Block())

        def make(e):
            def body(engine):
                waited = {}
                for o in self.ops[e]:
                    for (de, di) in o["deps"].items():
                        v = cnt[de][di]
                        key = "e" + de
                        if waited.get(key, 0) < v:
                            engine.wait_ge(esem[de], v)
                            waited[key] = v
                    for (ds, dc) in o["ddeps"]:
                        key = "d" + ds
                        v = 16 * dc
                        if waited.get(key, 0) < v:
                            engine.wait_ge(self.dsems[ds], v)
                            waited[key] = v
                    if o["fn"] is None:
                        continue
                    ins = o["fn"](engine)
                    if o["dsem"] is not None:
                        ins.then_inc(self.dsems[o["dsem"]], 16)
                    elif o["sig"]:
                        ins.then_inc(esem[e], 1)
            return body

        for e in ENGS:
            if self.ops[e]:
                getattr(block, BLOCK_ATTR[e])(make(e))

    def close(self):
        self.stack.close()


def ffn_slabs(wg, wu, wd):
    g = wg.reshape(KC, 128, 44, 128).transpose(2, 1, 0, 3)
    u = wu.reshape(KC, 128, 44, 128).transpose(2, 1, 0, 3)
    gu = np.stack([g, u], axis=2).reshape(44, 128, SLAB)
    dn = wd.reshape(11, 4, 128, 2, 1024).transpose(0, 3, 2, 1, 4).reshape(11, 2, 128, SLAB)
    out = []
    for gi in range(11):
        for fi in range(4):
            out.append(gu[4 * gi + fi])
        out.append(dn[gi, 0])
        out.append(dn[gi, 1])
    return out


def gla_slabs(w_in, w_out):
    out = []

    def cols(c0):
        return w_in[:, c0:c0 + 256].reshape(KC, 128, 256).transpose(1, 0, 2).reshape(128, SLAB)

    for hd in range(4):
        out.append(cols(hd * 256))
        out.append(cols(1024 + hd * 256))
        out.append(cols(2048 + hd * 512))
        out.append(cols(2048 + hd * 512 + 256))
        out.append(cols(4112 + hd * 512))
        out.append(cols(4112 + hd * 512 + 256))
        wo = w_out[hd * 512:(hd + 1) * 512].reshape(4, 128, 2, 1024).transpose(2, 1, 0, 3).reshape(2, 128, SLAB)
        out.append(wo[0])
        out.append(wo[1])
    return out


def pool_slabs(pw):
    out = []
    for s in range(2):
        out.append(pw[2 * s:2 * s + 2].reshape(2, 4, 128, 512).transpose(2, 0, 1, 3).reshape(128, SLAB))
    return out


N_SLABS = 66 * 4 + 32 + 2


STAGE_SLABS = {"n0": 5, "f0": 66, "n1": 66, "gla": 98, "n2": 98, "f1": 164, "n3": 164, "f2": 230, "pool": 232, "n5": 232, "f3": 298, None: 298}


def build(stop_after=None):
    N_SLABS = STAGE_SLABS[stop_after]
    nc = bass.Bass("TRN2", target_bir_lowering=False)
    xin = nc.dram_tensor("xin", [D, T], F32, kind="ExternalInput").ap()
    WS = nc.dram_tensor("WS", [N_SLABS, 128, SLAB], F32, kind="ExternalInput").ap()
    gn_d = nc.dram_tensor("gn", [128, 7 * 16], F32, kind="ExternalInput").ap()
    cst_d = nc.dram_tensor("cst", [128, 5 * 128], F32, kind="ExternalInput").ap()
    wlr_d = nc.dram_tensor("wlr", [128, 16 * 16], F32, kind="ExternalInput").ap()
    wlra_d = nc.dram_tensor("wlra", [17, 1024], F32, kind="ExternalInput").ap()
    hnb_d = nc.dram_tensor("hnb", [128, 512], F32, kind="ExternalInput").ap()
    pbs_d = nc.dram_tensor("pbs", [128, 32], F32, kind="ExternalInput").ap()
    sin_d = nc.dram_tensor("s_in", [4, 2, 128, 512], F32, kind="ExternalInput").ap()
    sout_d = nc.dram_tensor("s_out", [4, 2, 128, 512], F32, kind="ExternalOutput").ap()
    yout = nc.dram_tensor("yout", [D, 1024], F32, kind="ExternalOutput").ap()

    P = Prog(nc)
    x = P.sb("x", [128, KC, T], F32)
    h = P.sb("h", [128, KC, T], BF16)
    slots = [P.sb("slot%d" % i, [128, SLAB], BF16) for i in range(NS)]
    sqt = P.sb("sqt", [128, T], F32)
    sqacc = P.sb("sqacc", [128, T], F32)
    rstd = P.sb("rstd", [128, T], F32)
    gn = P.sb("gn_sb", [128, 7 * 16], F32)
    cst = P.sb("cst_sb", [128, 5 * 128], F32)
    ident = P.sb("ident", [128, 128], BF16)
    epsb = P.sb("epsb", [128, 1], F32)
    oneb = P.sb("oneb", [128, 1], F32)
    wlr = P.sb("wlr_sb", [128, 16 * 16], BF16)
    pbs = P.sb("pbs_sb", [128, 32], F32)
    UB = 51200
    U = P.sb("U", [128, UB // 2], BF16)
    pb = [P.ps("pb%d" % i, [128, 512], F32) for i in range(7)]
    pbt = P.ps("pbt", [128, 1024], BF16)

    ones = cst[:, 0:128]
    uneg = cst[:, 128:256]
    lneg = cst[:, 256:384]
    mask = cst[:, 384:512]
    identf = cst[:, 512:640]
    lr_aug = sqacc
    wlra = rstd
    hnb = sqt

    class Carver:
        def __init__(self):
            self.off = 0

        def take(self, shape, dt):
            n = int(np.prod(shape[1:]))
            nb = n * (2 if dt == BF16 else 4)
            nb = (nb + 63) // 64 * 64
            assert self.off + nb <= UB, (self.off, nb)
            a = U[:, self.off // 2:self.off // 2 + (n if dt == BF16 else 2 * n)]
            if dt == F32:
                a = a.bitcast(F32)
            self.off += nb
            if len(shape) == 3:
                a = a.rearrange("p (a b) -> p a b", a=shape[1])
            return a

    st = dict(next_use=0, next_dma=0)

    def acquire(n):
        j0 = st["next_use"]
        lim = min(j0 + NS, N_SLABS)
        while st["next_dma"] < lim:
            j = st["next_dma"]
            s = j % NS
            P.add("pool", lambda e, j=j, s=s: e.dma_start(out=slots[s][:], in_=WS[j]), w=[("slot", s)], dsem="s%d" % s)
            st["next_dma"] += 1
        st["next_use"] += n
        return [(slots[(j0 + i) % NS], ("slot", (j0 + i) % NS)) for i in range(n)]

    for c4 in range(4):
        P.add("sp", lambda e, c4=c4: e.dma_start(out=x[:, 4 * c4:4 * c4 + 4, :], in_=xin[512 * c4:512 * c4 + 512, :].rearrange("(c p) t -> p c t", p=128)),
              w=[("x", 4 * c4 + i, ti) for i in range(4) for ti in range(3)], dsem="x%d" % c4)
    P.add("sp", lambda e: e.dma_start(out=gn[:], in_=gn_d), w=["gn"], dsem="c0")
    P.add("sp", lambda e: e.dma_start(out=cst[:], in_=cst_d), w=["cst"], dsem="c1")
    P.add("sp", lambda e: e.dma_start(out=pbs[:], in_=pbs_d), w=["pbs"], dsem="c2")
    P.add("pool", lambda e: e.dma_start(out=wlr[:], in_=wlr_d), w=["wlr"], dsem="c3")
    P.add("dve", lambda e: e.memset(epsb[:], EPS), w=["eps"])
    P.add("dve", lambda e: e.memset(oneb[:], 1.0), w=["oneb"])
    P.add("dve", lambda e: e.tensor_copy(out=ident[:], in_=identf), r=["cst"], w=["ident"])

    bank_rr = dict(i=0)

    def next_bank():
        b = bank_rr["i"] % 7
        bank_rr["i"] += 1
        return b

    XK = lambda c: [("x", c, ti) for ti in range(3)]

    def rms_stats(src_keys_fn, src_ap_fn):
        for c in range(KC):
            if c == 0:
                P.add("act", lambda e: e.activation(out=sqacc[:], in_=src_ap_fn(0), func=AF.Square), r=src_keys_fn(0), w=["sqacc"])
            else:
                P.add("act", lambda e, c=c: e.activation(out=sqt[:], in_=src_ap_fn(c), func=AF.Square), r=src_keys_fn(c), w=["sqt"])
                P.add("dve", lambda e: e.tensor_tensor(out=sqacc[:], in0=sqacc[:], in1=sqt[:], op=ALU.add), r=["sqacc", "sqt"], w=["sqacc"])
        for ti, (t0, n) in enumerate(TT):
            b = next_bank()
            P.add("pe", lambda e, b=b, t0=t0, n=n: e.matmul(pb[b][:, 0:n], lhsT=ones, rhs=sqacc[:, t0:t0 + n], start=True, stop=True),
                  r=["cst", "sqacc"], w=[("pb", b)])
            P.add("act", lambda e, b=b, t0=t0, n=n: e.activation(out=rstd[:, t0:t0 + n], in_=pb[b][:, 0:n], func=AF.Ln, bias=epsb[:, 0:1]),
                  r=[("pb", b), "eps"], w=["rstd"])
            P.add("act", lambda e, t0=t0, n=n: e.activation(out=rstd[:, t0:t0 + n], in_=rstd[:, t0:t0 + n], func=AF.Exp, scale=-0.5),
                  r=["rstd"], w=["rstd"])

    def rmsnorm(i):
        rms_stats(XK, lambda c: x[:, c, :])
        for c in range(KC):
            P.add("dve", lambda e, c=c: e.scalar_tensor_tensor(out=h[:, c, :], in0=x[:, c, :], scalar=gn[:, i * 16 + c:i * 16 + c + 1], in1=rstd[:], op0=ALU.mult, op1=ALU.mult),
                  r=XK(c) + ["gn", "rstd"], w=[("h", c)])

    def ffn():
        cv = Carver()
        a = cv.take([128, 4, T], BF16)
        tmp = [cv.take([128, 512], F32) for _ in range(3)]
        for g in range(11):
            for fi in range(4):
                (slab, sk), = acquire(1)
                sv = slab[:].rearrange("p (w k m) -> p w k m", w=2, k=KC)
                for which in range(2):
                    for ti, (t0, n) in enumerate(TT):
                        b = which * 3 + ti
                        for kc in range(KC):
                            P.add("pe", lambda e, b=b, n=n, t0=t0, which=which, kc=kc, sv=sv: e.matmul(pb[b][:, 0:n], lhsT=sv[:, which, kc, :], rhs=h[:, kc, t0:t0 + n], start=(kc == 0), stop=(kc == KC - 1)),
                                  r=[sk, ("h", kc)], w=[("pb", b)])
                for ti, (t0, n) in enumerate(TT):
                    P.add("act", lambda e, ti=ti, n=n: e.activation(out=tmp[ti][:, 0:n], in_=pb[ti][:, 0:n], func=AF.Silu),
                          r=[("pb", ti)], w=[("tmp", ti)], ucls="ffn")
                    P.add("dve", lambda e, ti=ti, n=n, t0=t0, fi=fi: e.tensor_tensor(out=a[:, fi, t0:t0 + n], in0=tmp[ti][:, 0:n], in1=pb[3 + ti][:, 0:n], op=ALU.mult),
                          r=[("tmp", ti), ("pb", 3 + ti)], w=[("a", fi, ti)], ucls="ffn")
            for dh in range(2):
                (slab, sk), = acquire(1)
                sv = slab[:].rearrange("p (f m) -> p f m", f=4)
                for dd in range(8):
                    d = dh * 8 + dd
                    for ti, (t0, n) in enumerate(TT):
                        b = next_bank()
                        for fi in range(4):
                            P.add("pe", lambda e, b=b, n=n, t0=t0, fi=fi, dd=dd, sv=sv: e.matmul(pb[b][:, 0:n], lhsT=sv[:, fi, dd * 128:(dd + 1) * 128], rhs=a[:, fi, t0:t0 + n], start=(fi == 0), stop=(fi == 3)),
                                  r=[sk, ("a", fi, ti)], w=[("pb", b)], ucls="ffn")
                        P.add("dve", lambda e, b=b, n=n, t0=t0, d=d: e.scalar_tensor_tensor(out=x[:, d, t0:t0 + n], in0=pb[b][:, 0:n], scalar=0.5, in1=x[:, d, t0:t0 + n], op0=ALU.mult, op1=ALU.add),
                              r=[("pb", b), ("x", d, ti)], w=[("x", d, ti)])

    def gla():
        cv = Carver()
        qt_ = cv.take([128, 2, T], BF16)
        kt_ = cv.take([128, 2, T], BF16)
        khat = cv.take([128, NCH, 256], BF16)
        v = cv.take([128, NCH, 512], BF16)
        ogT = cv.take([128, 4, T], BF16)
        S_bf = cv.take([128, 2, 512], BF16)
        AT = [cv.take([128, 128], BF16) for _ in range(2)]
        S = cv.take([128, 2, 512], F32)
        sp_ = [cv.take([128, 256], F32) for _ in range(2)]
        eb = [cv.take([128, 2, 128], F32) for _ in range(2)]
        ed = [cv.take([128, 256], F32) for _ in range(2)]
        emb = [cv.take([128, 2, 128], F32) for _ in range(2)]
        dec = cv.take([128, 2, 16], F32)
        junk = [cv.take([128, 512], F32) for _ in range(2)]
        ss = cv.take([128, 4], F32)
        G = "gla"
        P.add("sp", lambda e: e.dma_start(out=wlra[0:17, 0:1024], in_=wlra_d), w=["rstd"], dsem="c4")
        P.add("sp", lambda e: e.dma_start(out=hnb[:, 0:512], in_=hnb_d), w=["sqt"], dsem="c5")
        P.add("dve", lambda e: e.memset(lr_aug[0:32, :], 1.0), w=["sqacc"])
        for ti, (t0, n) in enumerate(TT):
            b = next_bank()
            for kc in range(KC):
                P.add("pe", lambda e, b=b, n=n, t0=t0, kc=kc: e.matmul(pb[b][0:16, 0:n], lhsT=wlr[:, kc * 16:(kc + 1) * 16], rhs=h[:, kc, t0:t0 + n], start=(kc == 0), stop=(kc == KC - 1)),
                      r=["wlr", ("h", kc)], w=[("pb", b)])
            P.add("act", lambda e, b=b, n=n, t0=t0: e.activation(out=lr_aug[0:16, t0:t0 + n], in_=pb[b][0:16, 0:n], func=AF.Copy),
                  r=[("pb", b)], w=["sqacc"])

        def head(hd):
            (wq, kq), (wk, kk), (wv0, kv0), (wv1, kv1) = acquire(4)
            wqv = wq[:].rearrange("p (k m) -> p k m", k=KC)
            wkv = wk[:].rearrange("p (k m) -> p k m", k=KC)
            wvv = [wv0[:].rearrange("p (k m) -> p k m", k=KC), wv1[:].rearrange("p (k m) -> p k m", k=KC)]
            kvk = [kv0, kv1]
            P.add("sp", lambda e, hd=hd: e.dma_start(out=S[:], in_=sin_d[hd].rearrange("k p v -> p k v")), w=["S"], dsem="c6", ucls=G)
            P.add("act", lambda e: e.activation(out=S_bf[:], in_=S[:], func=AF.Copy), r=["S"], w=["Sbf"], ucls=G)

            def inproj(c):
                n = 128 if c < 8 else 16
                t0 = c * 128
                pz, pbT, pd, pq, pk, pkt, pv = 0, 1, 2, 3, 4, 5, 6
                i2 = c % 2
                P.add("pe", lambda e: e.matmul(pb[pz][0:n, 0:256], lhsT=lr_aug[0:17, t0:t0 + n], rhs=wlra[0:17, hd * 256:(hd + 1) * 256], start=True, stop=True),
                      r=["sqacc", "rstd"], w=[("pb", pz)])
                P.add("act", lambda e: e.activation(out=sp_[i2][0:n, :], in_=pb[pz][0:n, 0:256], func=AF.Exp, scale=-1.0), r=[("pb", pz)], w=[("sp", i2)], ucls=G)
                P.add("act", lambda e: e.activation(out=sp_[i2][0:n, :], in_=sp_[i2][0:n, :], func=AF.Ln, bias=oneb[0:n, 0:1]), r=[("sp", i2), "oneb"], w=[("sp", i2)], ucls=G)
                pbTv = pb[pbT][:, 0:256].rearrange("p (k t) -> p k t", k=2)
                for kc in range(2):
                    P.add("pe", lambda e, kc=kc: e.matmul(pbTv[:, kc, 0:n], lhsT=sp_[i2][0:n, kc * 128:(kc + 1) * 128], rhs=uneg[0:n, 0:n], start=True, stop=True),
                          r=[("sp", i2), "cst"], w=[("pb", pbT)], ucls=G)
                P.add("pe", lambda e: e.matmul(pb[pd][0:n, 0:256], lhsT=lneg[0:n, 0:n], rhs=sp_[i2][0:n, :], start=True, stop=True),
                      r=[("sp", i2), "cst"], w=[("pb", pd)], ucls=G)
                P.add("act", lambda e: e.activation(out=eb[i2][:, :, 0:n], in_=pbTv[:, :, 0:n], func=AF.Exp), r=[("pb", pbT)], w=[("eb", i2)], ucls=G)
                P.add("act", lambda e: e.activation(out=emb[i2][:, :, 0:n], in_=pbTv[:, :, 0:n], func=AF.Exp, scale=-1.0), r=[("pb", pbT)], w=[("emb", i2)], ucls=G)
                P.add("act", lambda e: e.activation(out=dec[:, :, c:c + 1], in_=pbTv[:, :, n - 1:n], func=AF.Exp), r=[("pb", pbT)], w=[("dec", c)], ucls=G)
                P.add("act", lambda e: e.activation(out=ed[i2][0:n, :], in_=pb[pd][0:n, 0:256], func=AF.Exp), r=[("pb", pd)], w=[("ed", i2)], ucls=G)
                pqv = pb[pq][:, 0:256].rearrange("p (k t) -> p k t", k=2)
                pkv = pb[pk][:, 0:256].rearrange("p (k t) -> p k t", k=2)
                for (pv_, wv_, wkey, bk) in ((pqv, wqv, kq, pq), (pkv, wkv, kk, pk)):
                    for fc in range(2):
                        for kc in range(KC):
                            P.add("pe", lambda e, pv_=pv_, wv_=wv_, fc=fc, kc=kc: e.matmul(pv_[:, fc, 0:n], lhsT=wv_[:, kc, fc * 128:(fc + 1) * 128], rhs=h[:, kc, t0:t0 + n], start=(kc == 0), stop=(kc == KC - 1)),
                                  r=[wkey, ("h", kc)], w=[("pb", bk)])
                P.add("dve", lambda e: e.scalar_tensor_tensor(out=qt_[:, :, t0:t0 + n], in0=pqv[:, :, 0:n], scalar=1.0 / 16.0, in1=eb[i2][:, :, 0:n], op0=ALU.mult, op1=ALU.mult),
                      r=[("pb", pq), ("eb", i2)], w=[("qt", c)], ucls=G)
                P.add("dve", lambda e: e.tensor_tensor(out=kt_[:, :, t0:t0 + n], in0=pkv[:, :, 0:n], in1=emb[i2][:, :, 0:n], op=ALU.mult),
                      r=[("pb", pk), ("emb", i2)], w=[("kt", c)], ucls=G)
                for kc in range(KC):
                    P.add("pe", lambda e, kc=kc: e.matmul(pb[pkt][0:n, 0:256], lhsT=h[:, kc, t0:t0 + n], rhs=wkv[:, kc, :], start=(kc == 0), stop=(kc == KC - 1)),
                          r=[kk, ("h", kc)], w=[("pb", pkt)])
                P.add("dve", lambda e: e.tensor_tensor(out=khat[0:n, c, :], in0=pb[pkt][0:n, 0:256], in1=ed[i2][0:n, :], op=ALU.mult),
                      r=[("pb", pkt), ("ed", i2)], w=[("khat", c)], ucls=G)
                for hf in range(2):
                    for kc in range(KC):
                        P.add("pe", lambda e, hf=hf, kc=kc: e.matmul(pb[pv][0:n, hf * 256:(hf + 1) * 256], lhsT=h[:, kc, t0:t0 + n], rhs=wvv[hf][:, kc, :], start=(kc == 0), stop=(kc == KC - 1)),
                              r=[kvk[hf], ("h", kc)], w=[("pb", pv)])
                P.add("act", lambda e: e.activation(out=v[0:n, c, :], in_=pb[pv][0:n, :], func=AF.Copy), r=[("pb", pv)], w=[("v", c)], ucls=G)

            def recur(c):
                n = 128 if c < 8 else 16
                t0 = c * 128
                i2 = c % 2
                pA, po, pS0, pS1 = 0, 1, 2, 3
                for kc in range(2):
                    P.add("pe", lambda e, kc=kc: e.matmul(pb[pA][0:n, 0:n], lhsT=kt_[:, kc, t0:t0 + n], rhs=qt_[:, kc, t0:t0 + n], start=(kc == 0), stop=(kc == 1)),
                          r=[("kt", c), ("qt", c)], w=[("pb", pA)], ucls=G)
                P.add("dve", lambda e: e.tensor_tensor(out=AT[i2][0:n, 0:n], in0=pb[pA][0:n, 0:n], in1=mask[0:n, 0:n], op=ALU.mult),
                      r=[("pb", pA), "cst"], w=[("AT", i2)], ucls=G)
                for kc in range(2):
                    P.add("pe", lambda e, kc=kc: e.matmul(pb[po][0:n, :], lhsT=qt_[:, kc, t0:t0 + n], rhs=S_bf[:, kc, :], start=(kc == 0), stop=False),
                          r=[("qt", c), "Sbf"], w=[("pb", po)], ucls=G)
                P.add("pe", lambda e: e.matmul(pb[po][0:n, :], lhsT=AT[i2][0:n, 0:n], rhs=v[0:n, c, :], start=False, stop=True),
                      r=[("AT", i2), ("v", c)], w=[("pb", po)], ucls=G)
                if c < 8:
                    for kc, pS in ((0, pS0), (1, pS1)):
                        P.add("pe", lambda e, kc=kc, pS=pS: e.matmul(pb[pS][:, :], lhsT=khat[0:n, c, kc * 128:(kc + 1) * 128], rhs=v[0:n, c, :], start=True, stop=True),
                              r=[("khat", c), ("v", c)], w=[("pb", pS)], ucls=G)
                        P.add("dve", lambda e, kc=kc, pS=pS: e.scalar_tensor_tensor(out=S[:, kc, :], in0=S[:, kc, :], scalar=dec[:, kc, c:c + 1], in1=pb[pS][:, :], op0=ALU.mult, op1=ALU.add),
                              r=[("pb", pS), "S", ("dec", c)], w=["S"], ucls=G)
                    P.add("act", lambda e: e.activation(out=S_bf[:], in_=S[:], func=AF.Copy), r=["S"], w=["Sbf"], ucls=G)
                    if c == 7:
                        P.add("sp", lambda e: e.dma_start(out=sout_d[hd].rearrange("k p v -> p k v"), in_=S[:]), r=["S"], w=[("sout", hd)], dsem="so", ucls=G)
                P.add("act", lambda e: e.activation(out=junk[i2][0:n, :], in_=pb[po][0:n, :], func=AF.Square), r=[("pb", po)], w=[("junk", i2)], ucls=G)
                P.add("dve", lambda e: e.reduce_sum(out=ss[0:n, i2:i2 + 1], in_=junk[i2][0:n, :], axis=AX.X), r=[("junk", i2)], w=[("ss", i2)], ucls=G)
                P.add("act", lambda e: e.activation(out=ss[0:n, i2:i2 + 1], in_=ss[0:n, i2:i2 + 1], func=AF.Ln, bias=epsb[0:n, 0:1], scale=1.0 / 512.0), r=[("ss", i2), "eps"], w=[("ss", i2)], ucls=G)
                P.add("act", lambda e: e.activation(out=ss[0:n, i2:i2 + 1], in_=ss[0:n, i2:i2 + 1], func=AF.Exp, scale=-0.5), r=[("ss", i2)], w=[("ss", i2)], ucls=G)
                P.add("dve", lambda e: e.scalar_tensor_tensor(out=v[0:n, c, :], in0=pb[po][0:n, :], scalar=ss[0:n, i2:i2 + 1], in1=hnb[0:n, 0:512], op0=ALU.mult, op1=ALU.mult),
                      r=[("pb", po), ("ss", i2), "sqt"], w=[("v", c)], ucls=G)

            for c in range(NCH):
                inproj(c)
                if c >= 1:
                    recur(c - 1)
            recur(NCH - 1)

            (wr0, kr0), (wr1, kr1) = acquire(2)
            wrv = [wr0[:].rearrange("p (k m) -> p k m", k=KC), wr1[:].rearrange("p (k m) -> p k m", k=KC)]
            krk = [kr0, kr1]
            ptv = pbt[:, 0:512].rearrange("p (k t) -> p k t", k=4)
            for c in range(NCH):
                n = 128 if c < 8 else 16
                t0 = c * 128
                i2 = c % 2
                pr = 4 + i2
                for hf in range(2):
                    for kc in range(KC):
                        P.add("pe", lambda e, hf=hf, kc=kc, n=n, t0=t0, pr=pr: e.matmul(pb[pr][0:n, hf * 256:(hf + 1) * 256], lhsT=h[:, kc, t0:t0 + n], rhs=wrv[hf][:, kc, :], start=(kc == 0), stop=(kc == KC - 1)),
                              r=[krk[hf], ("h", kc)], w=[("pb", pr)])
                P.add("act", lambda e, n=n, pr=pr, i2=i2: e.activation(out=junk[i2][0:n, :], in_=pb[pr][0:n, :], func=AF.Silu), r=[("pb", pr)], w=[("junk", i2)], ucls=G)
                P.add("dve", lambda e, n=n, c=c, i2=i2: e.tensor_tensor(out=v[0:n, c, :], in0=v[0:n, c, :], in1=junk[i2][0:n, :], op=ALU.mult), r=[("v", c), ("junk", i2)], w=[("v", c)], ucls=G)
                for vc in range(4):
                    P.add("pe", lambda e, n=n, c=c, vc=vc: e.transpose(out=ptv[:, vc, 0:n], in_=v[0:n, c, vc * 128:(vc + 1) * 128], identity=ident[0:n, 0:n]),
                          r=[("v", c), "ident"], w=["pbt"], ucls=G)
                P.add("act", lambda e, n=n, t0=t0: e.activation(out=ogT[:, :, t0:t0 + n], in_=ptv[:, :, 0:n], func=AF.Copy), r=["pbt"], w=[("ogT", c)], ucls=G)
            for dh in range(2):
                (wo, ko), = acquire(1)
                wov = wo[:].rearrange("p (f m) -> p f m", f=4)
                for dd in range(8):
                    d = dh * 8 + dd
                    for ti, (t0, n) in enumerate(TT):
                        b = next_bank()
                        cs_ = [c for c in range(NCH) if c * 128 < t0 + n and (c + 1) * 128 > t0]
                        for vc in range(4):
                            P.add("pe", lambda e, b=b, n=n, t0=t0, vc=vc, dd=dd, wov=wov: e.matmul(pb[b][:, 0:n], lhsT=wov[:, vc, dd * 128:(dd + 1) * 128], rhs=ogT[:, vc, t0:t0 + n], start=(vc == 0), stop=(vc == 3)),
                                  r=[ko] + [("ogT", c) for c in cs_], w=[("pb", b)], ucls=G)
                        P.add("dve", lambda e, b=b, n=n, t0=t0, d=d: e.tensor_tensor(out=x[:, d, t0:t0 + n], in0=x[:, d, t0:t0 + n], in1=pb[b][:, 0:n], op=ALU.add),
                              r=[("pb", b), ("x", d, ti)], w=[("x", d, ti)])

        for hd in range(4):
            head(hd)

    def pool_mixer(inorm):
        cv = Carver()
        pooled = cv.take([128, KC, T], BF16)
        hp = cv.take([128, T], F32)
        sa = cv.take([128, T], F32)
        sb_ = cv.take([128, T], F32)
        Q = "pool"
        rms_stats(XK, lambda c: x[:, c, :])
        for c in range(KC):
            win = WINS[c // 4]
            P.add("dve", lambda e, c=c: e.scalar_tensor_tensor(out=hp[:], in0=x[:, c, :], scalar=gn[:, inorm * 16 + c:inorm * 16 + c + 1], in1=rstd[:], op0=ALU.mult, op1=ALU.mult),
                  r=XK(c) + ["gn", "rstd"], w=["hp"], ucls=Q)
            src, skey = hp, "hp"
            bufs = [(sa, "sa"), (sb_, "sb")]
            sh = 1
            bi = 0
            while sh < win:
                dst, dkey = bufs[bi % 2]
                bi += 1
                P.add("dve", lambda e, src=src, dst=dst, sh=sh: e.tensor_tensor(out=dst[:, sh:T], in0=src[:, sh:T], in1=src[:, 0:T - sh], op=ALU.add),
                      r=[skey], w=[dkey], ucls=Q)
                P.add("dve", lambda e, src=src, dst=dst, sh=sh: e.tensor_copy(out=dst[:, 0:sh], in_=src[:, 0:sh]), r=[skey], w=[dkey], ucls=Q)
                src, skey = dst, dkey
                sh *= 2
            P.add("dve", lambda e, c=c, src=src, win=win: e.scalar_tensor_tensor(out=pooled[:, c, :], in0=src[:], scalar=1.0 / win, in1=hp[:], op0=ALU.mult, op1=ALU.subtract),
                  r=[skey, "hp"], w=[("pooled", c)], ucls=Q)
        for s in range(2):
            (wp, kp), = acquire(1)
            wpv = wp[:].rearrange("p (g k m) -> p g k m", g=2, k=4)
            for gi in range(2):
                g = 2 * s + gi
                for dc in range(4):
                    d = g * 4 + dc
                    for ti, (t0, n) in enumerate(TT):
                        b = next_bank()
                        for kc in range(4):
                            P.add("pe", lambda e, b=b, n=n, t0=t0, gi=gi, kc=kc, dc=dc, g=g, wpv=wpv: e.matmul(pb[b][:, 0:n], lhsT=wpv[:, gi, kc, dc * 128:(dc + 1) * 128], rhs=pooled[:, g * 4 + kc, t0:t0 + n], start=(kc == 0), stop=(kc == 3)),
                                  r=[kp, ("pooled", g * 4 + kc)], w=[("pb", b)], ucls=Q)
                        P.add("dve", lambda e, b=b, n=n, d=d: e.tensor_scalar(out=sa[:, 0:n], in0=pb[b][:, 0:n], scalar1=pbs[:, d:d + 1], scalar2=pbs[:, 16 + d:16 + d + 1], op0=ALU.add, op1=ALU.mult),
                              r=[("pb", b), "pbs"], w=["sa"], ucls=Q)
                        P.add("dve", lambda e, n=n, t0=t0, d=d: e.tensor_tensor(out=x[:, d, t0:t0 + n], in0=x[:, d, t0:t0 + n], in1=sa[:, 0:n], op=ALU.add),
                              r=["sa", ("x", d, ti)], w=[("x", d, ti)], ucls=Q)

    def final():
        rms_stats(XK, lambda c: x[:, c, :])
        for c in range(KC):
            P.add("dve", lambda e, c=c: e.scalar_tensor_tensor(out=x[:, c, :], in0=x[:, c, :], scalar=gn[:, 6 * 16 + c:6 * 16 + c + 1], in1=rstd[:], op0=ALU.mult, op1=ALU.mult),
                  r=XK(c) + ["gn", "rstd"], w=XK(c))
        dump()

    def dump():
        for c4 in range(4):
            P.add("sp", lambda e, c4=c4: e.dma_start(out=yout[512 * c4:512 * c4 + 512, :].rearrange("(c p) t -> p c t", p=128), in_=x[:, 4 * c4:4 * c4 + 4, 16:T]),
                  r=[k for i in range(4) for k in XK(4 * c4 + i)], w=[("yout", c4)], dsem="y%d" % c4)
        P.add("sp", None, r=[("yout", c4) for c4 in range(4)] + [("sout", hd) for hd in range(4)])

    stages = [
        ("n0", lambda: rmsnorm(0)), ("f0", ffn),
        ("n1", lambda: rmsnorm(1)), ("gla", gla),
        ("n2", lambda: rmsnorm(2)), ("f1", ffn),
        ("n3", lambda: rmsnorm(3)), ("f2", ffn),
        ("pool", lambda: pool_mixer(4)),
        ("n5", lambda: rmsnorm(5)), ("f3", ffn),
    ]
    done = False
    for name, fn in stages:
        fn()
        if stop_after == name:
            dump()
            done = True
            break
    if not done:
        final()
    P.emit()
    P.close()
    return nc


_NC_CACHE = {}


def _consts():
    j = np.arange(128)[:, None]
    i = np.arange(128)[None, :]
    ones = np.full((128, 128), 1.0 / D, np.float32)
    uneg = np.where(j <= i, -1.0 / 16.0, 0.0).astype(np.float32)
    lneg = np.where(j > i, -1.0 / 16.0, 0.0).astype(np.float32)
    mask = np.where(j <= i, 1.0, 0.0).astype(np.float32)
    ident = np.eye(128, dtype=np.float32)
    return np.ascontiguousarray(np.concatenate([ones, uneg, lneg, mask, ident], axis=1))


def prepare(x, meta, ffn_norm, ffn_w_gate, ffn_w_up, ffn_w_down, gla_norm, gla_w_in, gla_w_lr, gla_b_lr,
            gla_head_norm, gla_w_out, pool_norm, pool_w, pool_b, pool_scale, final_norm):
    f = lambda a: np.asarray(a, dtype=np.float32)
    x = f(x)
    meta = f(meta)
    slabs = []
    slabs += ffn_slabs(f(ffn_w_gate[0, 0]), f(ffn_w_up[0, 0]), f(ffn_w_down[0, 0]))
    slabs += gla_slabs(f(gla_w_in[0]), f(gla_w_out[0]))
    slabs += ffn_slabs(f(ffn_w_gate[0, 1]), f(ffn_w_up[0, 1]), f(ffn_w_down[0, 1]))
    slabs += ffn_slabs(f(ffn_w_gate[1, 0]), f(ffn_w_up[1, 0]), f(ffn_w_down[1, 0]))
    slabs += pool_slabs(f(pool_w[0]))
    slabs += ffn_slabs(f(ffn_w_gate[1, 1]), f(ffn_w_up[1, 1]), f(ffn_w_down[1, 1]))
    WS = np.ascontiguousarray(np.stack(slabs, axis=0))
    assert WS.shape[0] == N_SLABS
    gains = np.stack([f(ffn_norm[0, 0]), f(gla_norm[0]), f(ffn_norm[0, 1]), f(ffn_norm[1, 0]), f(pool_norm[0]), f(ffn_norm[1, 1]), f(final_norm)], axis=0)
    gn = np.ascontiguousarray(gains.reshape(7, KC, 128).transpose(2, 0, 1).reshape(128, 7 * 16))
    wlr = np.ascontiguousarray(f(gla_w_in[0])[:, 4096:4112].reshape(KC, 128, 16).transpose(1, 0, 2).reshape(128, 256))
    wlra = np.ascontiguousarray(np.concatenate([f(gla_w_lr[0]), f(gla_b_lr[0])[None, :]], axis=0))
    hnb = np.ascontiguousarray(np.broadcast_to(f(gla_head_norm[0])[None, :], (128, 512)))
    pbv = f(pool_b[0]).reshape(D).reshape(KC, 128).T
    psv = f(pool_scale[0]).reshape(KC, 128).T
    pbs = np.ascontiguousarray(np.concatenate([pbv, psv], axis=1))
    common = dict(WS=WS, gn=gn, cst=_consts(), wlr=wlr, wlra=wlra, hnb=hnb, pbs=pbs)
    xins = []
    for b in range(4):
        seq0 = np.concatenate([meta, x[b, 0:1024]], axis=0)
        seq1 = x[b, 1008:2048]
        xins.append(np.ascontiguousarray(seq0.T))
        xins.append(np.ascontiguousarray(seq1.T))
    return common, xins


def run(inputs, stop_after=None, trace=False):
    common, xins = prepare(**inputs)
    common["WS"] = np.ascontiguousarray(common["WS"][:STAGE_SLABS[stop_after]])
    if stop_after not in _NC_CACHE:
        _NC_CACHE[stop_after] = build(stop_after)
    nc = _NC_CACHE[stop_after]
    zero_s = np.zeros((4, 2, 128, 512), np.float32)
    in_maps = [dict(common, xin=xins[c], s_in=zero_s) for c in range(8)]
    res1 = run_bass_kernel_spmd(nc, in_maps, core_ids=list(range(8)), trace=trace)
    if stop_after in ("n0", "f0", "n1"):
        out = np.empty((4, 2048, D), np.float32)
        for c in range(8):
            b, hf = c // 2, c % 2
            out[b, hf * 1024:(hf + 1) * 1024, :] = res1.results[c]["yout"].T
        return out, (res1, res1)
    in_maps = []
    for c in range(8):
        s_in = zero_s if c % 2 == 0 else np.ascontiguousarray(res1.results[c - 1]["s_out"])
        in_maps.append(dict(common, xin=xins[c], s_in=s_in))
    res2 = run_bass_kernel_spmd(nc, in_maps, core_ids=list(range(8)), trace=trace)
    out = np.empty((4, 2048, D), np.float32)
    for c in range(8):
        b, hf = c // 2, c % 2
        out[b, hf * 1024:(hf + 1) * 1024, :] = res2.results[c]["yout"].T
    return out, (res1, res2)


def kernel(**inputs):
    out, _ = run(inputs)
    return out
```

```python
import contextlib
import numpy as np
import concourse.bass as bass
import concourse.mybir as mybir
from concourse.bass_utils import run_bass_kernel_spmd

F32 = mybir.dt.float32
BF16 = mybir.dt.bfloat16
AF = mybir.ActivationFunctionType
ALU = mybir.AluOpType
AX = mybir.AxisListType

D = 2048
KC = 16
FF = 5632
T = 1040
TT = [(0, 512), (512, 512), (1024, 16)]
NCH = 9
NS = 5
SLAB = 4096
EPS = 1e-6
WINS = (2, 4, 8, 16)

ENGS = ("pe", "act", "dve", "pool", "sp")
BLOCK_ATTR = {"pe": "tensor", "act": "scalar", "dve": "vector", "pool": "gpsimd", "sp": "sync"}


class Prog:
    def __init__(self, nc):
        self.nc = nc
        self.ops = {e: [] for e in ENGS}
        self.lastw = {}
        self.rd = {}
        self.dcount = {}
        self.stack = contextlib.ExitStack()
        self.dsems = {}
        self.ucur = None
        self.uops = ({}, set())
        self.ubar = ({}, set())

    def sb(self, name, shape, dt):
        return self.stack.enter_context(self.nc.sbuf_tensor(name, list(shape), dt))

    def ps(self, name, shape, dt):
        return self.stack.enter_context(self.nc.psum_tensor(name, list(shape), dt))

    def add(self, eng, fn, r=(), w=(), dsem=None, ucls=None):
        idx = len(self.ops[eng])
        me = (eng, idx)
        deps = {}
        ddeps = set()

        def need(x):
            if x is None or x == me:
                return
            e, i = x
            o = self.ops[e][i]
            if o["dsem"] is not None:
                ddeps.add((o["dsem"], o["dcnt"]))
            else:
                if e == "pe" and eng == "pe":
                    return
                if deps.get(e, -1) < i:
                    deps[e] = i

        for k in r:
            need(self.lastw.get(k))
        for k in w:
            need(self.lastw.get(k))
            for x in self.rd.get(k, ()):
                need(x)
        if ucls is not None:
            if ucls != self.ucur:
                self.ubar = (dict(self.uops[0]), set(self.uops[1]))
                self.uops = ({}, set())
                self.ucur = ucls
            for e, i in self.ubar[0].items():
                if not (e == "pe" and eng == "pe") and deps.get(e, -1) < i:
                    deps[e] = i
            ddeps |= self.ubar[1]
        dcnt = None
        if dsem is not None:
            dcnt = self.dcount.get(dsem, 0) + 1
            self.dcount[dsem] = dcnt
        self.ops[eng].append(dict(fn=fn, deps=deps, ddeps=ddeps, dsem=dsem, dcnt=dcnt, sig=False))
        if ucls is not None:
            if dsem is not None:
                self.uops[1].add((dsem, dcnt))
            else:
                self.uops[0][eng] = idx
        for k in r:
            self.rd.setdefault(k, []).append(me)
        for k in w:
            self.lastw[k] = me
            self.rd[k] = []
        return me

    def emit(self):
        nc = self.nc
        for e in ENGS:
            for o in self.ops[e]:
                for (de, di) in o["deps"].items():
                    self.ops[de][di]["sig"] = True
        cnt = {}
        for e in ENGS:
            c = 0
            arr = []
            for o in self.ops[e]:
                if o["sig"]:
                    c += 1
                arr.append(c)
            cnt[e] = arr
        esem = {e: self.stack.enter_context(nc.semaphore("es_" + e)) for e in ENGS}
        for name in self.dcount:
            self.dsems[name] = self.stack.enter_context(nc.semaphore("ds_" + name))
        block = self.stack.enter_context(nc.Block())

        def make(e):
            def body(engine):
                waited = {}
                for o in self.ops[e]:
                    for (de, di) in o["deps"].items():
                        v = cnt[de][di]
                        key = "e" + de
                        if waited.get(key, 0) < v:
                            engine.wait_ge(esem[de], v)
                            waited[key] = v
                    for (ds, dc) in o["ddeps"]:
                        key = "d" + ds
                        v = 16 * dc
                        if waited.get(key, 0) < v:
                            engine.wait_ge(self.dsems[ds], v)
                            waited[key] = v
                    if o["fn"] is None:
                        continue
                    ins = o["fn"](engine)
                    if o["dsem"] is not None:
                        ins.then_inc(self.dsems[o["dsem"]], 16)
                    elif o["sig"]:
                        ins.then_inc(esem[e], 1)
            return body

        for e in ENGS:
            if self.ops[e]:
                getattr(block, BLOCK_ATTR[e])(make(e))

    def close(self):
        self.stack.close()


def ffn_slabs(wg, wu, wd):
    g = wg.reshape(KC, 128, 44, 128).transpose(2, 1, 0, 3)
    u = wu.reshape(KC, 128, 44, 128).transpose(2, 1, 0, 3)
    gu = np.stack([g, u], axis=2).reshape(44, 128, SLAB)
    dn = wd.reshape(11, 4, 128, 2, 1024).transpose(0, 3, 2, 1, 4).reshape(11, 2, 128, SLAB)
    out = []
    for gi in range(11):
        for fi in range(4):
            out.append(gu[4 * gi + fi])
        out.append(dn[gi, 0])
        out.append(dn[gi, 1])
    return out


def gla_slabs(w_in, w_out):
    out = []

    def cols(c0):
        return w_in[:, c0:c0 + 256].reshape(KC, 128, 256).transpose(1, 0, 2).reshape(128, SLAB)

    for hd in range(4):
        out.append(cols(hd * 256))
        out.append(cols(1024 + hd * 256))
        out.append(cols(2048 + hd * 512))
        out.append(cols(2048 + hd * 512 + 256))
        out.append(cols(4112 + hd * 512))
        out.append(cols(4112 + hd * 512 + 256))
        wo = w_out[hd * 512:(hd + 1) * 512].reshape(4, 128, 2, 1024).transpose(2, 1, 0, 3).reshape(2, 128, SLAB)
        out.append(wo[0])
        out.append(wo[1])
    return out


def pool_slabs(pw):
    out = []
    for s in range(2):
        out.append(pw[2 * s:2 * s + 2].reshape(2, 4, 128, 512).transpose(2, 0, 1, 3).reshape(128, SLAB))
    return out


N_SLABS = 66 * 4 + 32 + 2


STAGE_SLABS = {"n0": 5, "f0": 66, "n1": 66, "gla": 98, "n2": 98, "f1": 164, "n3": 164, "f2": 230, "pool": 232, "n5": 232, "f3": 298, None: 298}


def build(stop_after=None):
    nc = bass.Bass("TRN2", target_bir_lowering=False)
    xin = nc.dram_tensor("xin", [D, T], F32, kind="ExternalInput").ap()
    WS = nc.dram_tensor("WS", [N_SLABS, 128, SLAB], F32, kind="ExternalInput").ap()
    gn_d = nc.dram_tensor("gn", [128, 7 * 16], F32, kind="ExternalInput").ap()
    cst_d = nc.dram_tensor("cst", [128, 5 * 128], F32, kind="ExternalInput").ap()
    wlr_d = nc.dram_tensor("wlr", [128, 16 * 16], F32, kind="ExternalInput").ap()
    wlra_d = nc.dram_tensor("wlra", [17, 1024], F32, kind="ExternalInput").ap()
    hnb_d = nc.dram_tensor("hnb", [128, 512], F32, kind="ExternalInput").ap()
    pbs_d = nc.dram_tensor("pbs", [128, 32], F32, kind="ExternalInput").ap()
    xpre = nc.dram_tensor("xpre", [D, T], F32, kind="ExternalInput").ap()
    sout_d = nc.dram_tensor("s_out", [4, 2, 128, 512], F32, kind="ExternalOutput").ap()
    yout = nc.dram_tensor("yout", [D, 1024], F32, kind="ExternalOutput").ap()

    P = Prog(nc)
    x = P.sb("x", [128, KC, T], F32)
    h = P.sb("h", [128, KC, T], BF16)
    slots = [P.sb("slot%d" % i, [128, SLAB], BF16) for i in range(NS)]
    sqt = P.sb("sqt", [128, T], F32)
    sqacc = P.sb("sqacc", [128, T], F32)
    rstd = P.sb("rstd", [128, T], F32)
    gn = P.sb("gn_sb", [128, 7 * 16], F32)
    cst = P.sb("cst_sb", [128, 5 * 128], F32)
    ident = P.sb("ident", [128, 128], BF16)
    epsb = P.sb("epsb", [128, 1], F32)
    oneb = P.sb("oneb", [128, 1], F32)
    wlr = P.sb("wlr_sb", [128, 16 * 16], BF16)
    pbs = P.sb("pbs_sb", [128, 32], F32)
    UB = 51200
    U = P.sb("U", [128, UB // 2], BF16)
    pb = [P.ps("pb%d" % i, [128, 512], F32) for i in range(7)]
    pbt = P.ps("pbt", [128, 1024], BF16)

    ones = cst[:, 0:128]
    uneg = cst[:, 128:256]
    lneg = cst[:, 256:384]
    mask = cst[:, 384:512]
    identf = cst[:, 512:640]
    lr_aug = sqacc
    wlra = rstd
    hnb = sqt

    class Carver:
        def __init__(self):
            self.off = 0

        def take(self, shape, dt):
            n = int(np.prod(shape[1:]))
            nb = n * (2 if dt == BF16 else 4)
            nb = (nb + 63) // 64 * 64
            assert self.off + nb <= UB, (self.off, nb)
            a = U[:, self.off // 2:self.off // 2 + (n if dt == BF16 else 2 * n)]
            if dt == F32:
                a = a.bitcast(F32)
            self.off += nb
            if len(shape) == 3:
                a = a.rearrange("p (a b) -> p a b", a=shape[1])
            return a

    st = dict(next_use=0, next_dma=0)
    SEQ = list(range(66))
    for hd_ in range(4):
        SEQ += [66 + hd_ * 8 + 1, 66 + hd_ * 8 + 2, 66 + hd_ * 8 + 3]
    SEQ += list(range(N_SLABS))

    def acquire(n):
        j0 = st["next_use"]
        lim = min(j0 + NS, len(SEQ))
        while st["next_dma"] < lim:
            j = st["next_dma"]
            s = j % NS
            P.add("pool", lambda e, j=j, s=s: e.dma_start(out=slots[s][:], in_=WS[SEQ[j]]), w=[("slot", s)], dsem="s%d" % s)
            st["next_dma"] += 1
        st["next_use"] += n
        return [(slots[(j0 + i) % NS], ("slot", (j0 + i) % NS)) for i in range(n)]

    def load_x(src):
        for c4 in range(4):
            P.add("sp", lambda e, c4=c4: e.dma_start(out=x[:, 4 * c4:4 * c4 + 4, :], in_=src[512 * c4:512 * c4 + 512, :].rearrange("(c p) t -> p c t", p=128)),
                  w=[("x", 4 * c4 + i, ti) for i in range(4) for ti in range(3)], dsem="x%d" % c4)

    load_x(xpre)
    P.add("sp", lambda e: e.dma_start(out=gn[:], in_=gn_d), w=["gn"], dsem="c0")
    P.add("sp", lambda e: e.dma_start(out=cst[:], in_=cst_d), w=["cst"], dsem="c1")
    P.add("sp", lambda e: e.dma_start(out=pbs[:], in_=pbs_d), w=["pbs"], dsem="c2")
    P.add("pool", lambda e: e.dma_start(out=wlr[:], in_=wlr_d), w=["wlr"], dsem="c3")
    P.add("dve", lambda e: e.memset(epsb[:], EPS), w=["eps"])
    P.add("dve", lambda e: e.memset(oneb[:], 1.0), w=["oneb"])
    P.add("dve", lambda e: e.tensor_copy(out=ident[:], in_=identf), r=["cst"], w=["ident"])

    bank_rr = dict(i=0)

    def next_bank():
        b = bank_rr["i"] % 7
        bank_rr["i"] += 1
        return b

    XK = lambda c: [("x", c, ti) for ti in range(3)]

    def rms_stats(src_keys_fn, src_ap_fn):
        for c in range(KC):
            if c == 0:
                P.add("act", lambda e: e.activation(out=sqacc[:], in_=src_ap_fn(0), func=AF.Square), r=src_keys_fn(0), w=["sqacc"])
            else:
                P.add("act", lambda e, c=c: e.activation(out=sqt[:], in_=src_ap_fn(c), func=AF.Square), r=src_keys_fn(c), w=["sqt"])
                P.add("dve", lambda e: e.tensor_tensor(out=sqacc[:], in0=sqacc[:], in1=sqt[:], op=ALU.add), r=["sqacc", "sqt"], w=["sqacc"])
        for ti, (t0, n) in enumerate(TT):
            b = next_bank()
            P.add("pe", lambda e, b=b, t0=t0, n=n: e.matmul(pb[b][:, 0:n], lhsT=ones, rhs=sqacc[:, t0:t0 + n], start=True, stop=True),
                  r=["cst", "sqacc"], w=[("pb", b)])
            P.add("act", lambda e, b=b, t0=t0, n=n: e.activation(out=rstd[:, t0:t0 + n], in_=pb[b][:, 0:n], func=AF.Ln, bias=epsb[:, 0:1]),
                  r=[("pb", b), "eps"], w=["rstd"])
            P.add("act", lambda e, t0=t0, n=n: e.activation(out=rstd[:, t0:t0 + n], in_=rstd[:, t0:t0 + n], func=AF.Exp, scale=-0.5),
                  r=["rstd"], w=["rstd"])

    def rmsnorm(i):
        rms_stats(XK, lambda c: x[:, c, :])
        for c in range(KC):
            P.add("dve", lambda e, c=c: e.scalar_tensor_tensor(out=h[:, c, :], in0=x[:, c, :], scalar=gn[:, i * 16 + c:i * 16 + c + 1], in1=rstd[:], op0=ALU.mult, op1=ALU.mult),
                  r=XK(c) + ["gn", "rstd"], w=[("h", c)])

    def ffn():
        cv = Carver()
        a = cv.take([128, 4, T], BF16)
        tmp = [cv.take([128, 512], F32) for _ in range(3)]
        for g in range(11):
            for fi in range(4):
                (slab, sk), = acquire(1)
                sv = slab[:].rearrange("p (w k m) -> p w k m", w=2, k=KC)
                for which in range(2):
                    for ti, (t0, n) in enumerate(TT):
                        b = which * 3 + ti
                        for kc in range(KC):
                            P.add("pe", lambda e, b=b, n=n, t0=t0, which=which, kc=kc, sv=sv: e.matmul(pb[b][:, 0:n], lhsT=sv[:, which, kc, :], rhs=h[:, kc, t0:t0 + n], start=(kc == 0), stop=(kc == KC - 1)),
                                  r=[sk, ("h", kc)], w=[("pb", b)])
                for ti, (t0, n) in enumerate(TT):
                    P.add("act", lambda e, ti=ti, n=n: e.activation(out=tmp[ti][:, 0:n], in_=pb[ti][:, 0:n], func=AF.Silu),
                          r=[("pb", ti)], w=[("tmp", ti)], ucls="ffn")
                    P.add("dve", lambda e, ti=ti, n=n, t0=t0, fi=fi: e.tensor_tensor(out=a[:, fi, t0:t0 + n], in0=tmp[ti][:, 0:n], in1=pb[3 + ti][:, 0:n], op=ALU.mult),
                          r=[("tmp", ti), ("pb", 3 + ti)], w=[("a", fi, ti)], ucls="ffn")
            for dh in range(2):
                (slab, sk), = acquire(1)
                sv = slab[:].rearrange("p (f m) -> p f m", f=4)
                for dd in range(8):
                    d = dh * 8 + dd
                    for ti, (t0, n) in enumerate(TT):
                        b = next_bank()
                        for fi in range(4):
                            P.add("pe", lambda e, b=b, n=n, t0=t0, fi=fi, dd=dd, sv=sv: e.matmul(pb[b][:, 0:n], lhsT=sv[:, fi, dd * 128:(dd + 1) * 128], rhs=a[:, fi, t0:t0 + n], start=(fi == 0), stop=(fi == 3)),
                                  r=[sk, ("a", fi, ti)], w=[("pb", b)], ucls="ffn")
                        P.add("dve", lambda e, b=b, n=n, t0=t0, d=d: e.scalar_tensor_tensor(out=x[:, d, t0:t0 + n], in0=pb[b][:, 0:n], scalar=0.5, in1=x[:, d, t0:t0 + n], op0=ALU.mult, op1=ALU.add),
                              r=[("pb", b), ("x", d, ti)], w=[("x", d, ti)])

    def gla():
        cv = Carver()
        qt_ = cv.take([128, 2, T], BF16)
        kt_ = cv.take([128, 2, T], BF16)
        khat = cv.take([128, NCH, 256], BF16)
        v = cv.take([128, NCH, 512], BF16)
        ogT = cv.take([128, 4, T], BF16)
        S_bf = cv.take([128, 2, 512], BF16)
        AT = [cv.take([128, 128], BF16) for _ in range(2)]
        S = cv.take([128, 2, 512], F32)
        sp_ = [cv.take([128, 256], F32) for _ in range(2)]
        eb = [cv.take([128, 2, 128], F32) for _ in range(2)]
        ed = [cv.take([128, 256], F32) for _ in range(2)]
        emb = [cv.take([128, 2, 128], F32) for _ in range(2)]
        dec = cv.take([128, 2, 16], F32)
        junk = [cv.take([128, 512], F32) for _ in range(2)]
        ss = cv.take([128, 4], F32)
        G = "gla"
        P.add("sp", lambda e: e.dma_start(out=wlra[0:17, 0:1024], in_=wlra_d), w=["rstd"], dsem="c4")
        P.add("sp", lambda e: e.dma_start(out=hnb[:, 0:512], in_=hnb_d), w=["sqt"], dsem="c5")
        P.add("dve", lambda e: e.memset(lr_aug[0:32, :], 1.0), w=["sqacc"])
        for ti, (t0, n) in enumerate(TT):
            b = next_bank()
            for kc in range(KC):
                P.add("pe", lambda e, b=b, n=n, t0=t0, kc=kc: e.matmul(pb[b][0:16, 0:n], lhsT=wlr[:, kc * 16:(kc + 1) * 16], rhs=h[:, kc, t0:t0 + n], start=(kc == 0), stop=(kc == KC - 1)),
                      r=["wlr", ("h", kc)], w=[("pb", b)])
            P.add("act", lambda e, b=b, n=n, t0=t0: e.activation(out=lr_aug[0:16, t0:t0 + n], in_=pb[b][0:16, 0:n], func=AF.Copy),
                  r=[("pb", b)], w=["sqacc"])

        def head(hd):
            (wq, kq), (wk, kk), (wv0, kv0), (wv1, kv1) = acquire(4)
            wqv = wq[:].rearrange("p (k m) -> p k m", k=KC)
            wkv = wk[:].rearrange("p (k m) -> p k m", k=KC)
            wvv = [wv0[:].rearrange("p (k m) -> p k m", k=KC), wv1[:].rearrange("p (k m) -> p k m", k=KC)]
            kvk = [kv0, kv1]
            P.add("sp", lambda e, hd=hd: e.dma_start(out=S[:], in_=sout_d[hd].rearrange("k p v -> p k v")), r=[("sout", hd)], w=["S"], dsem="c6", ucls=G)
            P.add("act", lambda e: e.activation(out=S_bf[:], in_=S[:], func=AF.Copy), r=["S"], w=["Sbf"], ucls=G)

            def inproj(c):
                n = 128 if c < 8 else 16
                t0 = c * 128
                pz, pbT, pd, pq, pk, pkt, pv = 0, 1, 2, 3, 4, 5, 6
                i2 = c % 2
                P.add("pe", lambda e: e.matmul(pb[pz][0:n, 0:256], lhsT=lr_aug[0:17, t0:t0 + n], rhs=wlra[0:17, hd * 256:(hd + 1) * 256], start=True, stop=True),
                      r=["sqacc", "rstd"], w=[("pb", pz)])
                P.add("act", lambda e: e.activation(out=sp_[i2][0:n, :], in_=pb[pz][0:n, 0:256], func=AF.Exp, scale=-1.0), r=[("pb", pz)], w=[("sp", i2)], ucls=G)
                P.add("act", lambda e: e.activation(out=sp_[i2][0:n, :], in_=sp_[i2][0:n, :], func=AF.Ln, bias=oneb[0:n, 0:1]), r=[("sp", i2), "oneb"], w=[("sp", i2)], ucls=G)
                pbTv = pb[pbT][:, 0:256].rearrange("p (k t) -> p k t", k=2)
                for kc in range(2):
                    P.add("pe", lambda e, kc=kc: e.matmul(pbTv[:, kc, 0:n], lhsT=sp_[i2][0:n, kc * 128:(kc + 1) * 128], rhs=uneg[0:n, 0:n], start=True, stop=True),
                          r=[("sp", i2), "cst"], w=[("pb", pbT)], ucls=G)
                P.add("pe", lambda e: e.matmul(pb[pd][0:n, 0:256], lhsT=lneg[0:n, 0:n], rhs=sp_[i2][0:n, :], start=True, stop=True),
                      r=[("sp", i2), "cst"], w=[("pb", pd)], ucls=G)
                P.add("act", lambda e: e.activation(out=eb[i2][:, :, 0:n], in_=pbTv[:, :, 0:n], func=AF.Exp), r=[("pb", pbT)], w=[("eb", i2)], ucls=G)
                P.add("act", lambda e: e.activation(out=emb[i2][:, :, 0:n], in_=pbTv[:, :, 0:n], func=AF.Exp, scale=-1.0), r=[("pb", pbT)], w=[("emb", i2)], ucls=G)
                P.add("act", lambda e: e.activation(out=dec[:, :, c:c + 1], in_=pbTv[:, :, n - 1:n], func=AF.Exp), r=[("pb", pbT)], w=[("dec", c)], ucls=G)
                P.add("act", lambda e: e.activation(out=ed[i2][0:n, :], in_=pb[pd][0:n, 0:256], func=AF.Exp), r=[("pb", pd)], w=[("ed", i2)], ucls=G)
                pqv = pb[pq][:, 0:256].rearrange("p (k t) -> p k t", k=2)
                pkv = pb[pk][:, 0:256].rearrange("p (k t) -> p k t", k=2)
                for (pv_, wv_, wkey, bk) in ((pqv, wqv, kq, pq), (pkv, wkv, kk, pk)):
                    for fc in range(2):
                        for kc in range(KC):
                            P.add("pe", lambda e, pv_=pv_, wv_=wv_, fc=fc, kc=kc: e.matmul(pv_[:, fc, 0:n], lhsT=wv_[:, kc, fc * 128:(fc + 1) * 128], rhs=h[:, kc, t0:t0 + n], start=(kc == 0), stop=(kc == KC - 1)),
                                  r=[wkey, ("h", kc)], w=[("pb", bk)])
                P.add("dve", lambda e: e.scalar_tensor_tensor(out=qt_[:, :, t0:t0 + n], in0=pqv[:, :, 0:n], scalar=1.0 / 16.0, in1=eb[i2][:, :, 0:n], op0=ALU.mult, op1=ALU.mult),
                      r=[("pb", pq), ("eb", i2)], w=[("qt", c)], ucls=G)
                P.add("dve", lambda e: e.tensor_tensor(out=kt_[:, :, t0:t0 + n], in0=pkv[:, :, 0:n], in1=emb[i2][:, :, 0:n], op=ALU.mult),
                      r=[("pb", pk), ("emb", i2)], w=[("kt", c)], ucls=G)
                for kc in range(KC):
                    P.add("pe", lambda e, kc=kc: e.matmul(pb[pkt][0:n, 0:256], lhsT=h[:, kc, t0:t0 + n], rhs=wkv[:, kc, :], start=(kc == 0), stop=(kc == KC - 1)),
                          r=[kk, ("h", kc)], w=[("pb", pkt)])
                P.add("dve", lambda e: e.tensor_tensor(out=khat[0:n, c, :], in0=pb[pkt][0:n, 0:256], in1=ed[i2][0:n, :], op=ALU.mult),
                      r=[("pb", pkt), ("ed", i2)], w=[("khat", c)], ucls=G)
                for hf in range(2):
                    for kc in range(KC):
                        P.add("pe", lambda e, hf=hf, kc=kc: e.matmul(pb[pv][0:n, hf * 256:(hf + 1) * 256], lhsT=h[:, kc, t0:t0 + n], rhs=wvv[hf][:, kc, :], start=(kc == 0), stop=(kc == KC - 1)),
                              r=[kvk[hf], ("h", kc)], w=[("pb", pv)])
                P.add("act", lambda e: e.activation(out=v[0:n, c, :], in_=pb[pv][0:n, :], func=AF.Copy), r=[("pb", pv)], w=[("v", c)], ucls=G)

            def recur(c):
                n = 128 if c < 8 else 16
                t0 = c * 128
                i2 = c % 2
                pA, po, pS0, pS1 = 0, 1, 2, 3
                for kc in range(2):
                    P.add("pe", lambda e, kc=kc: e.matmul(pb[pA][0:n, 0:n], lhsT=kt_[:, kc, t0:t0 + n], rhs=qt_[:, kc, t0:t0 + n], start=(kc == 0), stop=(kc == 1)),
                          r=[("kt", c), ("qt", c)], w=[("pb", pA)], ucls=G)
                P.add("dve", lambda e: e.tensor_tensor(out=AT[i2][0:n, 0:n], in0=pb[pA][0:n, 0:n], in1=mask[0:n, 0:n], op=ALU.mult),
                      r=[("pb", pA), "cst"], w=[("AT", i2)], ucls=G)
                for kc in range(2):
                    P.add("pe", lambda e, kc=kc: e.matmul(pb[po][0:n, :], lhsT=qt_[:, kc, t0:t0 + n], rhs=S_bf[:, kc, :], start=(kc == 0), stop=False),
                          r=[("qt", c), "Sbf"], w=[("pb", po)], ucls=G)
                P.add("pe", lambda e: e.matmul(pb[po][0:n, :], lhsT=AT[i2][0:n, 0:n], rhs=v[0:n, c, :], start=False, stop=True),
                      r=[("AT", i2), ("v", c)], w=[("pb", po)], ucls=G)
                if c < 8:
                    for kc, pS in ((0, pS0), (1, pS1)):
                        P.add("pe", lambda e, kc=kc, pS=pS: e.matmul(pb[pS][:, :], lhsT=khat[0:n, c, kc * 128:(kc + 1) * 128], rhs=v[0:n, c, :], start=True, stop=True),
                              r=[("khat", c), ("v", c)], w=[("pb", pS)], ucls=G)
                        P.add("dve", lambda e, kc=kc, pS=pS: e.scalar_tensor_tensor(out=S[:, kc, :], in0=S[:, kc, :], scalar=dec[:, kc, c:c + 1], in1=pb[pS][:, :], op0=ALU.mult, op1=ALU.add),
                              r=[("pb", pS), "S", ("dec", c)], w=["S"], ucls=G)
                    P.add("act", lambda e: e.activation(out=S_bf[:], in_=S[:], func=AF.Copy), r=["S"], w=["Sbf"], ucls=G)
                P.add("act", lambda e: e.activation(out=junk[i2][0:n, :], in_=pb[po][0:n, :], func=AF.Square), r=[("pb", po)], w=[("junk", i2)], ucls=G)
                P.add("dve", lambda e: e.reduce_sum(out=ss[0:n, i2:i2 + 1], in_=junk[i2][0:n, :], axis=AX.X), r=[("junk", i2)], w=[("ss", i2)], ucls=G)
                P.add("act", lambda e: e.activation(out=ss[0:n, i2:i2 + 1], in_=ss[0:n, i2:i2 + 1], func=AF.Ln, bias=epsb[0:n, 0:1], scale=1.0 / 512.0), r=[("ss", i2), "eps"], w=[("ss", i2)], ucls=G)
                P.add("act", lambda e: e.activation(out=ss[0:n, i2:i2 + 1], in_=ss[0:n, i2:i2 + 1], func=AF.Exp, scale=-0.5), r=[("ss", i2)], w=[("ss", i2)], ucls=G)
                P.add("dve", lambda e: e.scalar_tensor_tensor(out=v[0:n, c, :], in0=pb[po][0:n, :], scalar=ss[0:n, i2:i2 + 1], in1=hnb[0:n, 0:512], op0=ALU.mult, op1=ALU.mult),
                      r=[("pb", po), ("ss", i2), "sqt"], w=[("v", c)], ucls=G)

            for c in range(NCH):
                inproj(c)
                if c >= 1:
                    recur(c - 1)
            recur(NCH - 1)

            (wr0, kr0), (wr1, kr1) = acquire(2)
            wrv = [wr0[:].rearrange("p (k m) -> p k m", k=KC), wr1[:].rearrange("p (k m) -> p k m", k=KC)]
            krk = [kr0, kr1]
            ptv = pbt[:, 0:512].rearrange("p (k t) -> p k t", k=4)
            for c in range(NCH):
                n = 128 if c < 8 else 16
                t0 = c * 128
                i2 = c % 2
                pr = 4 + i2
                for hf in range(2):
                    for kc in range(KC):
                        P.add("pe", lambda e, hf=hf, kc=kc, n=n, t0=t0, pr=pr: e.matmul(pb[pr][0:n, hf * 256:(hf + 1) * 256], lhsT=h[:, kc, t0:t0 + n], rhs=wrv[hf][:, kc, :], start=(kc == 0), stop=(kc == KC - 1)),
                              r=[krk[hf], ("h", kc)], w=[("pb", pr)])
                P.add("act", lambda e, n=n, pr=pr, i2=i2: e.activation(out=junk[i2][0:n, :], in_=pb[pr][0:n, :], func=AF.Silu), r=[("pb", pr)], w=[("junk", i2)], ucls=G)
                P.add("dve", lambda e, n=n, c=c, i2=i2: e.tensor_tensor(out=v[0:n, c, :], in0=v[0:n, c, :], in1=junk[i2][0:n, :], op=ALU.mult), r=[("v", c), ("junk", i2)], w=[("v", c)], ucls=G)
                for vc in range(4):
                    P.add("pe", lambda e, n=n, c=c, vc=vc: e.transpose(out=ptv[:, vc, 0:n], in_=v[0:n, c, vc * 128:(vc + 1) * 128], identity=ident[0:n, 0:n]),
                          r=[("v", c), "ident"], w=["pbt"], ucls=G)
                P.add("act", lambda e, n=n, t0=t0: e.activation(out=ogT[:, :, t0:t0 + n], in_=ptv[:, :, 0:n], func=AF.Copy), r=["pbt"], w=[("ogT", c)], ucls=G)
            for dh in range(2):
                (wo, ko), = acquire(1)
                wov = wo[:].rearrange("p (f m) -> p f m", f=4)
                for dd in range(8):
                    d = dh * 8 + dd
                    for ti, (t0, n) in enumerate(TT):
                        b = next_bank()
                        cs_ = [c for c in range(NCH) if c * 128 < t0 + n and (c + 1) * 128 > t0]
                        for vc in range(4):
                            P.add("pe", lambda e, b=b, n=n, t0=t0, vc=vc, dd=dd, wov=wov: e.matmul(pb[b][:, 0:n], lhsT=wov[:, vc, dd * 128:(dd + 1) * 128], rhs=ogT[:, vc, t0:t0 + n], start=(vc == 0), stop=(vc == 3)),
                                  r=[ko] + [("ogT", c) for c in cs_], w=[("pb", b)], ucls=G)
                        P.add("dve", lambda e, b=b, n=n, t0=t0, d=d: e.tensor_tensor(out=x[:, d, t0:t0 + n], in0=x[:, d, t0:t0 + n], in1=pb[b][:, 0:n], op=ALU.add),
                              r=[("pb", b), ("x", d, ti)], w=[("x", d, ti)])

        for hd in range(4):
            head(hd)

    def gla_prefix():
        cv = Carver()
        khat2 = cv.take([128, 2, 256], BF16)
        v2 = cv.take([128, 2, 512], BF16)
        S = cv.take([128, 2, 512], F32)
        sp_ = [cv.take([128, 256], F32) for _ in range(2)]
        ed = [cv.take([128, 256], F32) for _ in range(2)]
        dec = cv.take([128, 2, 16], F32)
        G = "glap"
        P.add("sp", lambda e: e.dma_start(out=wlra[0:17, 0:1024], in_=wlra_d), w=["rstd"], dsem="c7")
        P.add("dve", lambda e: e.memset(lr_aug[0:32, :], 1.0), w=["sqacc"])
        for ti, (t0, n) in enumerate(TT):
            b = next_bank()
            for kc in range(KC):
                P.add("pe", lambda e, b=b, n=n, t0=t0, kc=kc: e.matmul(pb[b][0:16, 0:n], lhsT=wlr[:, kc * 16:(kc + 1) * 16], rhs=h[:, kc, t0:t0 + n], start=(kc == 0), stop=(kc == KC - 1)),
                      r=["wlr", ("h", kc)], w=[("pb", b)])
            P.add("act", lambda e, b=b, n=n, t0=t0: e.activation(out=lr_aug[0:16, t0:t0 + n], in_=pb[b][0:16, 0:n], func=AF.Copy),
                  r=[("pb", b)], w=["sqacc"])

        def phead(hd):
            (wk, kk), (wv0, kv0), (wv1, kv1) = acquire(3)
            wkv = wk[:].rearrange("p (k m) -> p k m", k=KC)
            wvv = [wv0[:].rearrange("p (k m) -> p k m", k=KC), wv1[:].rearrange("p (k m) -> p k m", k=KC)]
            kvk = [kv0, kv1]
            P.add("dve", lambda e: e.memset(S[:], 0.0), w=["S"], ucls=G)

            def pchunk(c):
                n = 128
                t0 = c * 128
                pz, pbT, pd, pkt, pv = 0, 1, 2, 5, 6
                i2 = c % 2
                P.add("pe", lambda e: e.matmul(pb[pz][0:n, 0:256], lhsT=lr_aug[0:17, t0:t0 + n], rhs=wlra[0:17, hd * 256:(hd + 1) * 256], start=True, stop=True),
                      r=["sqacc", "rstd"], w=[("pb", pz)])
                P.add("act", lambda e: e.activation(out=sp_[i2][0:n, :], in_=pb[pz][0:n, 0:256], func=AF.Exp, scale=-1.0), r=[("pb", pz)], w=[("sp", i2)], ucls=G)
                P.add("act", lambda e: e.activation(out=sp_[i2][0:n, :], in_=sp_[i2][0:n, :], func=AF.Ln, bias=oneb[0:n, 0:1]), r=[("sp", i2), "oneb"], w=[("sp", i2)], ucls=G)
                pbTv = pb[pbT][:, 0:256].rearrange("p (k t) -> p k t", k=2)
                for kc in range(2):
                    P.add("pe", lambda e, kc=kc: e.matmul(pbTv[:, kc, 0:n], lhsT=sp_[i2][0:n, kc * 128:(kc + 1) * 128], rhs=uneg[0:n, 0:n], start=True, stop=True),
                          r=[("sp", i2), "cst"], w=[("pb", pbT)], ucls=G)
                P.add("pe", lambda e: e.matmul(pb[pd][0:n, 0:256], lhsT=lneg[0:n, 0:n], rhs=sp_[i2][0:n, :], start=True, stop=True),
                      r=[("sp", i2), "cst"], w=[("pb", pd)], ucls=G)
                P.add("act", lambda e: e.activation(out=dec[:, :, c:c + 1], in_=pbTv[:, :, n - 1:n], func=AF.Exp), r=[("pb", pbT)], w=[("dec", c)], ucls=G)
                P.add("act", lambda e: e.activation(out=ed[i2][0:n, :], in_=pb[pd][0:n, 0:256], func=AF.Exp), r=[("pb", pd)], w=[("ed", i2)], ucls=G)
                for kc in range(KC):
                    P.add("pe", lambda e, kc=kc: e.matmul(pb[pkt][0:n, 0:256], lhsT=h[:, kc, t0:t0 + n], rhs=wkv[:, kc, :], start=(kc == 0), stop=(kc == KC - 1)),
                          r=[kk, ("h", kc)], w=[("pb", pkt)])
                P.add("dve", lambda e: e.tensor_tensor(out=khat2[0:n, i2, :], in0=pb[pkt][0:n, 0:256], in1=ed[i2][0:n, :], op=ALU.mult),
                      r=[("pb", pkt), ("ed", i2)], w=[("khat", i2)], ucls=G)
                for hf in range(2):
                    for kc in range(KC):
                        P.add("pe", lambda e, hf=hf, kc=kc: e.matmul(pb[pv][0:n, hf * 256:(hf + 1) * 256], lhsT=h[:, kc, t0:t0 + n], rhs=wvv[hf][:, kc, :], start=(kc == 0), stop=(kc == KC - 1)),
                              r=[kvk[hf], ("h", kc)], w=[("pb", pv)])
                P.add("act", lambda e: e.activation(out=v2[0:n, i2, :], in_=pb[pv][0:n, :], func=AF.Copy), r=[("pb", pv)], w=[("v", i2)], ucls=G)

            def pupd(c):
                n = 128
                i2 = c % 2
                for kc, pS in ((0, 3), (1, 4)):
                    P.add("pe", lambda e, kc=kc, pS=pS: e.matmul(pb[pS][:, :], lhsT=khat2[0:n, i2, kc * 128:(kc + 1) * 128], rhs=v2[0:n, i2, :], start=True, stop=True),
                          r=[("khat", i2), ("v", i2)], w=[("pb", pS)], ucls=G)
                    P.add("dve", lambda e, kc=kc, pS=pS: e.scalar_tensor_tensor(out=S[:, kc, :], in0=S[:, kc, :], scalar=dec[:, kc, c:c + 1], in1=pb[pS][:, :], op0=ALU.mult, op1=ALU.add),
                          r=[("pb", pS), "S", ("dec", c)], w=["S"], ucls=G)

            for c in range(8):
                pchunk(c)
                if c >= 1:
                    pupd(c - 1)
            pupd(7)
            P.add("sp", lambda e: e.dma_start(out=sout_d[hd].rearrange("k p v -> p k v"), in_=S[:]), r=["S"], w=[("sout", hd)], dsem="so", ucls=G)

        for hd in range(4):
            phead(hd)

    def pool_mixer(inorm):
        cv = Carver()
        pooled = cv.take([128, KC, T], BF16)
        hp = cv.take([128, T], F32)
        sa = cv.take([128, T], F32)
        sb_ = cv.take([128, T], F32)
        Q = "pool"
        rms_stats(XK, lambda c: x[:, c, :])
        for c in range(KC):
            win = WINS[c // 4]
            P.add("dve", lambda e, c=c: e.scalar_tensor_tensor(out=hp[:], in0=x[:, c, :], scalar=gn[:, inorm * 16 + c:inorm * 16 + c + 1], in1=rstd[:], op0=ALU.mult, op1=ALU.mult),
                  r=XK(c) + ["gn", "rstd"], w=["hp"], ucls=Q)
            src, skey = hp, "hp"
            bufs = [(sa, "sa"), (sb_, "sb")]
            sh = 1
            bi = 0
            while sh < win:
                dst, dkey = bufs[bi % 2]
                bi += 1
                P.add("dve", lambda e, src=src, dst=dst, sh=sh: e.tensor_tensor(out=dst[:, sh:T], in0=src[:, sh:T], in1=src[:, 0:T - sh], op=ALU.add),
                      r=[skey], w=[dkey], ucls=Q)
                P.add("dve", lambda e, src=src, dst=dst, sh=sh: e.tensor_copy(out=dst[:, 0:sh], in_=src[:, 0:sh]), r=[skey], w=[dkey], ucls=Q)
                src, skey = dst, dkey
                sh *= 2
            P.add("dve", lambda e, c=c, src=src, win=win: e.scalar_tensor_tensor(out=pooled[:, c, :], in0=src[:], scalar=1.0 / win, in1=hp[:], op0=ALU.mult, op1=ALU.subtract),
                  r=[skey, "hp"], w=[("pooled", c)], ucls=Q)
        for s in range(2):
            (wp, kp), = acquire(1)
            wpv = wp[:].rearrange("p (g k m) -> p g k m", g=2, k=4)
            for gi in range(2):
                g = 2 * s + gi
                for dc in range(4):
                    d = g * 4 + dc
                    for ti, (t0, n) in enumerate(TT):
                        b = next_bank()
                        for kc in range(4):
                            P.add("pe", lambda e, b=b, n=n, t0=t0, gi=gi, kc=kc, dc=dc, g=g, wpv=wpv: e.matmul(pb[b][:, 0:n], lhsT=wpv[:, gi, kc, dc * 128:(dc + 1) * 128], rhs=pooled[:, g * 4 + kc, t0:t0 + n], start=(kc == 0), stop=(kc == 3)),
                                  r=[kp, ("pooled", g * 4 + kc)], w=[("pb", b)], ucls=Q)
                        P.add("dve", lambda e, b=b, n=n, d=d: e.tensor_scalar(out=sa[:, 0:n], in0=pb[b][:, 0:n], scalar1=pbs[:, d:d + 1], scalar2=pbs[:, 16 + d:16 + d + 1], op0=ALU.add, op1=ALU.mult),
                              r=[("pb", b), "pbs"], w=["sa"], ucls=Q)
                        P.add("dve", lambda e, n=n, t0=t0, d=d: e.tensor_tensor(out=x[:, d, t0:t0 + n], in0=x[:, d, t0:t0 + n], in1=sa[:, 0:n], op=ALU.add),
                              r=["sa", ("x", d, ti)], w=[("x", d, ti)], ucls=Q)

    def final():
        rms_stats(XK, lambda c: x[:, c, :])
        for c in range(KC):
            P.add("dve", lambda e, c=c: e.scalar_tensor_tensor(out=x[:, c, :], in0=x[:, c, :], scalar=gn[:, 6 * 16 + c:6 * 16 + c + 1], in1=rstd[:], op0=ALU.mult, op1=ALU.mult),
                  r=XK(c) + ["gn", "rstd"], w=XK(c))
        dump()

    def dump():
        for c4 in range(4):
            P.add("sp", lambda e, c4=c4: e.dma_start(out=yout[512 * c4:512 * c4 + 512, :].rearrange("(c p) t -> p c t", p=128), in_=x[:, 4 * c4:4 * c4 + 4, 16:T]),
                  r=[k for i in range(4) for k in XK(4 * c4 + i)], w=[("yout", c4)], dsem="y%d" % c4)
        P.add("sp", None, r=[("yout", c4) for c4 in range(4)] + [("sout", hd) for hd in range(4)])

    rmsnorm(0)
    ffn()
    rmsnorm(1)
    gla_prefix()
    load_x(xin)
    stages = [
        ("n0", lambda: rmsnorm(0)), ("f0", ffn),
        ("n1", lambda: rmsnorm(1)), ("gla", gla),
        ("n2", lambda: rmsnorm(2)), ("f1", ffn),
        ("n3", lambda: rmsnorm(3)), ("f2", ffn),
        ("pool", lambda: pool_mixer(4)),
        ("n5", lambda: rmsnorm(5)), ("f3", ffn),
    ]
    done = False
    for name, fn in stages:
        fn()
        if stop_after == name:
            dump()
            done = True
            break
    if not done:
        final()
    P.emit()
    P.close()
    return nc


_NC_CACHE = {}


def _consts():
    j = np.arange(128)[:, None]
    i = np.arange(128)[None, :]
    ones = np.full((128, 128), 1.0 / D, np.float32)
    uneg = np.where(j <= i, -1.0 / 16.0, 0.0).astype(np.float32)
    lneg = np.where(j > i, -1.0 / 16.0, 0.0).astype(np.float32)
    mask = np.where(j <= i, 1.0, 0.0).astype(np.float32)
    ident = np.eye(128, dtype=np.float32)
    return np.ascontiguousarray(np.concatenate([ones, uneg, lneg, mask, ident], axis=1))


def prepare(x, meta, ffn_norm, ffn_w_gate, ffn_w_up, ffn_w_down, gla_norm, gla_w_in, gla_w_lr, gla_b_lr,
            gla_head_norm, gla_w_out, pool_norm, pool_w, pool_b, pool_scale, final_norm):
    f = lambda a: np.asarray(a, dtype=np.float32)
    x = f(x)
    meta = f(meta)
    slabs = []
    slabs += ffn_slabs(f(ffn_w_gate[0, 0]), f(ffn_w_up[0, 0]), f(ffn_w_down[0, 0]))
    slabs += gla_slabs(f(gla_w_in[0]), f(gla_w_out[0]))
    slabs += ffn_slabs(f(ffn_w_gate[0, 1]), f(ffn_w_up[0, 1]), f(ffn_w_down[0, 1]))
    slabs += ffn_slabs(f(ffn_w_gate[1, 0]), f(ffn_w_up[1, 0]), f(ffn_w_down[1, 0]))
    slabs += pool_slabs(f(pool_w[0]))
    slabs += ffn_slabs(f(ffn_w_gate[1, 1]), f(ffn_w_up[1, 1]), f(ffn_w_down[1, 1]))
    WS = np.ascontiguousarray(np.stack(slabs, axis=0))
    assert WS.shape[0] == N_SLABS
    gains = np.stack([f(ffn_norm[0, 0]), f(gla_norm[0]), f(ffn_norm[0, 1]), f(ffn_norm[1, 0]), f(pool_norm[0]), f(ffn_norm[1, 1]), f(final_norm)], axis=0)
    gn = np.ascontiguousarray(gains.reshape(7, KC, 128).transpose(2, 0, 1).reshape(128, 7 * 16))
    wlr = np.ascontiguousarray(f(gla_w_in[0])[:, 4096:4112].reshape(KC, 128, 16).transpose(1, 0, 2).reshape(128, 256))
    wlra = np.ascontiguousarray(np.concatenate([f(gla_w_lr[0]), f(gla_b_lr[0])[None, :]], axis=0))
    hnb = np.ascontiguousarray(np.broadcast_to(f(gla_head_norm[0])[None, :], (128, 512)))
    pbv = f(pool_b[0]).reshape(D).reshape(KC, 128).T
    psv = f(pool_scale[0]).reshape(KC, 128).T
    pbs = np.ascontiguousarray(np.concatenate([pbv, psv], axis=1))
    common = dict(WS=WS, gn=gn, cst=_consts(), wlr=wlr, wlra=wlra, hnb=hnb, pbs=pbs)
    xins = []
    xpres = []
    zpre = np.zeros((D, T), np.float32)
    for b in range(4):
        seq0 = np.concatenate([meta, x[b, 0:1024]], axis=0)
        seq1 = x[b, 1008:2048]
        xins.append(np.ascontiguousarray(seq0.T))
        xins.append(np.ascontiguousarray(seq1.T))
        pre1 = np.zeros((D, T), np.float32)
        pre1[:, 0:1024] = np.concatenate([meta, x[b, 0:1008]], axis=0).T
        xpres.append(zpre)
        xpres.append(pre1)
    return common, xins, xpres


def run(inputs, stop_after=None, trace=False):
    common, xins, xpres = prepare(**inputs)
    if stop_after not in _NC_CACHE:
        _NC_CACHE[stop_after] = build(stop_after)
    nc = _NC_CACHE[stop_after]
    in_maps = [dict(common, xin=xins[c], xpre=xpres[c]) for c in range(8)]
    res = run_bass_kernel_spmd(nc, in_maps, core_ids=list(range(8)), trace=trace)
    out = np.empty((4, 2048, D), np.float32)
    for c in range(8):
        b, hf = c // 2, c % 2
        out[b, hf * 1024:(hf + 1) * 1024, :] = res.results[c]["yout"].T
    return out, (res, res)


def kernel(**inputs):
    out, _ = run(inputs)
    return out
```

```python
import contextlib
import numpy as np
import concourse.bass as bass
import concourse.mybir as mybir
from concourse.bass_utils import run_bass_kernel_spmd

F32 = mybir.dt.float32
BF16 = mybir.dt.bfloat16
AF = mybir.ActivationFunctionType
ALU = mybir.AluOpType
AX = mybir.AxisListType

D = 2048
KC = 16
FF = 5632
T = 1040
TT = [(0, 512), (512, 512), (1024, 16)]
NCH = 9
NS = 5
SLAB = 4096
EPS = 1e-6
WINS = (2, 4, 8, 16)

ENGS = ("pe", "act", "dve", "pool", "sp")
BLOCK_ATTR = {"pe": "tensor", "act": "scalar", "dve": "vector", "pool": "gpsimd", "sp": "sync"}


class Prog:
    def __init__(self, nc):
        self.nc = nc
        self.ops = {e: [] for e in ENGS}
        self.lastw = {}
        self.rd = {}
        self.dcount = {}
        self.stack = contextlib.ExitStack()
        self.dsems = {}
        self.ucur = None
        self.uops = ({}, set())
        self.ubar = ({}, set())

    def sb(self, name, shape, dt):
        return self.stack.enter_context(self.nc.sbuf_tensor(name, list(shape), dt))

    def ps(self, name, shape, dt):
        return self.stack.enter_context(self.nc.psum_tensor(name, list(shape), dt))

    def add(self, eng, fn, r=(), w=(), dsem=None, ucls=None):
        idx = len(self.ops[eng])
        me = (eng, idx)
        deps = {}
        ddeps = set()

        def need(x):
            if x is None or x == me:
                return
            e, i = x
            o = self.ops[e][i]
            if o["dsem"] is not None:
                ddeps.add((o["dsem"], o["dcnt"]))
            else:
                if e == "pe" and eng == "pe":
                    return
                if deps.get(e, -1) < i:
                    deps[e] = i

        for k in r:
            need(self.lastw.get(k))
        for k in w:
            need(self.lastw.get(k))
            for x in self.rd.get(k, ()):
                need(x)
        if ucls is not None:
            if ucls != self.ucur:
                self.ubar = (dict(self.uops[0]), set(self.uops[1]))
                self.uops = ({}, set())
                self.ucur = ucls
            for e, i in self.ubar[0].items():
                if not (e == "pe" and eng == "pe") and deps.get(e, -1) < i:
                    deps[e] = i
            ddeps |= self.ubar[1]
        dcnt = None
        if dsem is not None:
            dcnt = self.dcount.get(dsem, 0) + 1
            self.dcount[dsem] = dcnt
        self.ops[eng].append(dict(fn=fn, deps=deps, ddeps=ddeps, dsem=dsem, dcnt=dcnt, sig=False))
        if ucls is not None:
            if dsem is not None:
                self.uops[1].add((dsem, dcnt))
            else:
                self.uops[0][eng] = idx
        for k in r:
            self.rd.setdefault(k, []).append(me)
        for k in w:
            self.lastw[k] = me
            self.rd[k] = []
        return me

    def emit(self):
        nc = self.nc
        for e in ENGS:
            for o in self.ops[e]:
                for (de, di) in o["deps"].items():
                    self.ops[de][di]["sig"] = True
        cnt = {}
        for e in ENGS:
            c = 0
            arr = []
            for o in self.ops[e]:
                if o["sig"]:
                    c += 1
                arr.append(c)
            cnt[e] = arr
        esem = {e: self.stack.enter_context(nc.semaphore("es_" + e)) for e in ENGS}
        for name in self.dcount:
            self.dsems[name] = self.stack.enter_context(nc.semaphore("ds_" + name))
        block = self.stack.enter_context(nc.Block())

        def make(e):
            def body(engine):
                waited = {}
                for o in self.ops[e]:
                    for (de, di) in o["deps"].items():
                        v = cnt[de][di]
                        key = "e" + de
                        if waited.get(key, 0) < v:
                            engine.wait_ge(esem[de], v)
                            waited[key] = v
                    for (ds, dc) in o["ddeps"]:
                        key = "d" + ds
                        v = 16 * dc
                        if waited.get(key, 0) < v:
                            engine.wait_ge(self.dsems[ds], v)
                            waited[key] = v
                    if o["fn"] is None:
                        continue
                    ins = o["fn"](engine)
                    if o["dsem"] is not None:
                        ins.then_inc(self.dsems[o["dsem"]], 16)
                    elif o["sig"]:
                        ins.then_inc(esem[e], 1)
            return body

        for e in ENGS:
            if self.ops[e]:
                getattr(block, BLOCK_ATTR[e])(make(e))

    def close(self):
        self.stack.close()


def ffn_slabs(wg, wu, wd):
    g = wg.reshape(KC, 128, 44, 128).transpose(2, 1, 0, 3)
    u = wu.reshape(KC, 128, 44, 128).transpose(2, 1, 0, 3)
    gu = np.stack([g, u], axis=2).reshape(44, 128, SLAB)
    dn = wd.reshape(11, 4, 128, 2, 1024).transpose(0, 3, 2, 1, 4).reshape(11, 2, 128, SLAB)
    out = []
    for gi in range(11):
        for fi in range(4):
            out.append(gu[4 * gi + fi])
        out.append(dn[gi, 0])
        out.append(dn[gi, 1])
    return out


def gla_slabs(w_in, w_out):
    out = []

    def cols(c0):
        return w_in[:, c0:c0 + 256].reshape(KC, 128, 256).transpose(1, 0, 2).reshape(128, SLAB)

    for hd in range(4):
        out.append(cols(hd * 256))
        out.append(cols(1024 + hd * 256))
        out.append(cols(2048 + hd * 512))
        out.append(cols(2048 + hd * 512 + 256))
        out.append(cols(4112 + hd * 512))
        out.append(cols(4112 + hd * 512 + 256))
        wo = w_out[hd * 512:(hd + 1) * 512].reshape(4, 128, 2, 1024).transpose(2, 1, 0, 3).reshape(2, 128, SLAB)
        out.append(wo[0])
        out.append(wo[1])
    return out


def pool_slabs(pw):
    out = []
    for s in range(2):
        out.append(pw[2 * s:2 * s + 2].reshape(2, 4, 128, 512).transpose(2, 0, 1, 3).reshape(128, SLAB))
    return out


N_SLABS = 66 * 4 + 32 + 2


STAGE_SLABS = {"n0": 5, "f0": 66, "n1": 66, "gla": 98, "n2": 98, "f1": 164, "n3": 164, "f2": 230, "pool": 232, "n5": 232, "f3": 298, None: 298}


def build(stop_after=None):
    nc = bass.Bass("TRN2", target_bir_lowering=False)
    xin = nc.dram_tensor("xin", [D, T], F32, kind="ExternalInput").ap()
    WS = nc.dram_tensor("WS", [N_SLABS, 128, SLAB], F32, kind="ExternalInput").ap()
    gn_d = nc.dram_tensor("gn", [128, 7 * 16], F32, kind="ExternalInput").ap()
    cst_d = nc.dram_tensor("cst", [128, 5 * 128], F32, kind="ExternalInput").ap()
    wlr_d = nc.dram_tensor("wlr", [128, 16 * 16], F32, kind="ExternalInput").ap()
    wlra_d = nc.dram_tensor("wlra", [17, 1024], F32, kind="ExternalInput").ap()
    hnb_d = nc.dram_tensor("hnb", [128, 512], F32, kind="ExternalInput").ap()
    pbs_d = nc.dram_tensor("pbs", [128, 32], F32, kind="ExternalInput").ap()
    xpre = nc.dram_tensor("xpre", [D, T], F32, kind="ExternalInput").ap()
    sout_d = nc.dram_tensor("s_out", [4, 2, 128, 512], F32, kind="ExternalOutput").ap()
    yout = nc.dram_tensor("yout", [D, 1024], F32, kind="ExternalOutput").ap()

    P = Prog(nc)
    x = P.sb("x", [128, KC, T], F32)
    h = P.sb("h", [128, KC, T], BF16)
    slots = [P.sb("slot%d" % i, [128, SLAB], BF16) for i in range(NS)]
    sqt = P.sb("sqt", [128, T], F32)
    sqacc = P.sb("sqacc", [128, T], F32)
    rstd = P.sb("rstd", [128, T], F32)
    gn = P.sb("gn_sb", [128, 7 * 16], F32)
    cst = P.sb("cst_sb", [128, 5 * 128], F32)
    ident = P.sb("ident", [128, 128], BF16)
    epsb = P.sb("epsb", [128, 1], F32)
    oneb = P.sb("oneb", [128, 1], F32)
    wlr = P.sb("wlr_sb", [128, 16 * 16], BF16)
    pbs = P.sb("pbs_sb", [128, 32], F32)
    UB = 51200
    U = P.sb("U", [128, UB // 2], BF16)
    pb = [P.ps("pb%d" % i, [128, 512], F32) for i in range(8)]
    pbt = pb[7][:].bitcast(BF16)
    sqt2 = P.sb("sqt2", [128, T], F32)

    ones = cst[:, 0:128]
    uneg = cst[:, 128:256]
    lneg = cst[:, 256:384]
    mask = cst[:, 384:512]
    identf = cst[:, 512:640]
    lr_aug = sqacc
    wlra = rstd
    hnb = sqt

    class Carver:
        def __init__(self):
            self.off = 0

        def take(self, shape, dt):
            n = int(np.prod(shape[1:]))
            nb = n * (2 if dt == BF16 else 4)
            nb = (nb + 63) // 64 * 64
            assert self.off + nb <= UB, (self.off, nb)
            a = U[:, self.off // 2:self.off // 2 + (n if dt == BF16 else 2 * n)]
            if dt == F32:
                a = a.bitcast(F32)
            self.off += nb
            if len(shape) == 3:
                a = a.rearrange("p (a b) -> p a b", a=shape[1])
            return a

    st = dict(next_use=0, next_dma=0)
    SEQ = list(range(66))
    for hd_ in range(4):
        SEQ += [66 + hd_ * 8 + 1, 66 + hd_ * 8 + 2, 66 + hd_ * 8 + 3]
    SEQ += list(range(N_SLABS))

    def acquire(n):
        j0 = st["next_use"]
        lim = min(j0 + NS, len(SEQ))
        while st["next_dma"] < lim:
            j = st["next_dma"]
            s = j % NS
            P.add("pool", lambda e, j=j, s=s: e.dma_start(out=slots[s][:], in_=WS[SEQ[j]]), w=[("slot", s)], dsem="s%d" % s)
            st["next_dma"] += 1
        st["next_use"] += n
        return [(slots[(j0 + i) % NS], ("slot", (j0 + i) % NS)) for i in range(n)]

    def load_x(src):
        for c4 in range(4):
            P.add("sp", lambda e, c4=c4: e.dma_start(out=x[:, 4 * c4:4 * c4 + 4, :], in_=src[512 * c4:512 * c4 + 512, :].rearrange("(c p) t -> p c t", p=128)),
                  w=[("x", 4 * c4 + i, ti) for i in range(4) for ti in range(3)], dsem="x%d" % c4)

    load_x(xpre)
    P.add("sp", lambda e: e.dma_start(out=gn[:], in_=gn_d), w=["gn"], dsem="c0")
    P.add("sp", lambda e: e.dma_start(out=cst[:], in_=cst_d), w=["cst"], dsem="c1")
    P.add("sp", lambda e: e.dma_start(out=pbs[:], in_=pbs_d), w=["pbs"], dsem="c2")
    P.add("pool", lambda e: e.dma_start(out=wlr[:], in_=wlr_d), w=["wlr"], dsem="c3")
    P.add("dve", lambda e: e.memset(epsb[:], EPS), w=["eps"])
    P.add("dve", lambda e: e.memset(oneb[:], 1.0), w=["oneb"])
    P.add("dve", lambda e: e.tensor_copy(out=ident[:], in_=identf), r=["cst"], w=["ident"])

    bank_rr = dict(i=0)

    def next_bank():
        b = bank_rr["i"] % 7
        bank_rr["i"] += 1
        return b

    XK = lambda c: [("x", c, ti) for ti in range(3)]
    TTM = [(16, 512), (528, 512)]

    def xk(d, t0, n):
        return [("x", d, ti) for ti, (a0, an) in enumerate(TT) if a0 < t0 + n and a0 + an > t0]

    sqbufs = [sqt, sqt2]
    sqkeys = ["sqt", "sqt2"]

    def stats_chunk(c):
        if c == 0:
            P.add("act", lambda e: e.activation(out=sqacc[:], in_=x[:, 0, :], func=AF.Square), r=XK(0), w=["sqacc"])
        else:
            P.add("act", lambda e, c=c: e.activation(out=sqbufs[c % 2][:], in_=x[:, c, :], func=AF.Square), r=XK(c), w=[sqkeys[c % 2]])
            P.add("pool", lambda e, c=c: e.tensor_tensor(out=sqacc[:], in0=sqacc[:], in1=sqbufs[c % 2][:], op=ALU.add), r=["sqacc", sqkeys[c % 2]], w=["sqacc"])

    def stats_finish(tiles):
        for (t0, n) in tiles:
            b = next_bank()
            P.add("pe", lambda e, b=b, t0=t0, n=n: e.matmul(pb[b][:, 0:n], lhsT=ones, rhs=sqacc[:, t0:t0 + n], start=True, stop=True),
                  r=["cst", "sqacc"], w=[("pb", b)])
            P.add("act", lambda e, b=b, t0=t0, n=n: e.activation(out=rstd[:, t0:t0 + n], in_=pb[b][:, 0:n], func=AF.Ln, bias=epsb[:, 0:1]),
                  r=[("pb", b), "eps"], w=["rstd"])
            P.add("act", lambda e, t0=t0, n=n: e.activation(out=rstd[:, t0:t0 + n], in_=rstd[:, t0:t0 + n], func=AF.Exp, scale=-0.5),
                  r=["rstd"], w=["rstd"])

    def rms_stats(tiles, pre_done=False):
        if not pre_done:
            for c in range(KC):
                stats_chunk(c)
        stats_finish(tiles)

    def rmsnorm(i, tiles=TT, pre_done=False):
        rms_stats(tiles, pre_done)
        lo = tiles[0][0]
        hi = tiles[-1][0] + tiles[-1][1]
        for c in range(KC):
            eng = "dve"
            P.add(eng, lambda e, c=c: e.scalar_tensor_tensor(out=h[:, c, lo:hi], in0=x[:, c, lo:hi], scalar=gn[:, i * 16 + c:i * 16 + c + 1], in1=rstd[:, lo:hi], op0=ALU.mult, op1=ALU.mult),
                  r=XK(c) + ["gn", "rstd"], w=[("h", c)])

    def ffn(tiles=TT, tail_cb=None):
        cv = Carver()
        a = cv.take([128, 4, T], BF16)
        tmp = [cv.take([128, 512], F32) for _ in range(3)]
        for g in range(11):
            for fi in range(4):
                (slab, sk), = acquire(1)
                sv = slab[:].rearrange("p (w k m) -> p w k m", w=2, k=KC)
                for which in range(2):
                    for ti, (t0, n) in enumerate(tiles):
                        b = which * 3 + ti
                        for kc in range(KC):
                            P.add("pe", lambda e, b=b, n=n, t0=t0, which=which, kc=kc, sv=sv: e.matmul(pb[b][:, 0:n], lhsT=sv[:, which, kc, :], rhs=h[:, kc, t0:t0 + n], start=(kc == 0), stop=(kc == KC - 1)),
                                  r=[sk, ("h", kc)], w=[("pb", b)])
                for ti, (t0, n) in enumerate(tiles):
                    P.add("act", lambda e, ti=ti, n=n: e.activation(out=tmp[ti][:, 0:n], in_=pb[ti][:, 0:n], func=AF.Silu),
                          r=[("pb", ti)], w=[("tmp", ti)], ucls="ffn")
                    P.add("dve", lambda e, ti=ti, n=n, t0=t0, fi=fi: e.tensor_tensor(out=a[:, fi, t0:t0 + n], in0=tmp[ti][:, 0:n], in1=pb[3 + ti][:, 0:n], op=ALU.mult),
                          r=[("tmp", ti), ("pb", 3 + ti)], w=[("a", fi, ti)], ucls="ffn")
            for dh in range(2):
                (slab, sk), = acquire(1)
                sv = slab[:].rearrange("p (f m) -> p f m", f=4)
                for dd in range(8):
                    d = dh * 8 + dd
                    for ti, (t0, n) in enumerate(tiles):
                        b = next_bank()
                        for fi in range(4):
                            P.add("pe", lambda e, b=b, n=n, t0=t0, fi=fi, dd=dd, sv=sv: e.matmul(pb[b][:, 0:n], lhsT=sv[:, fi, dd * 128:(dd + 1) * 128], rhs=a[:, fi, t0:t0 + n], start=(fi == 0), stop=(fi == 3)),
                                  r=[sk, ("a", fi, ti)], w=[("pb", b)], ucls="ffn")
                        P.add("dve", lambda e, b=b, n=n, t0=t0, d=d: e.scalar_tensor_tensor(out=x[:, d, t0:t0 + n], in0=pb[b][:, 0:n], scalar=0.5, in1=x[:, d, t0:t0 + n], op0=ALU.mult, op1=ALU.add),
                              r=[("pb", b)] + xk(d, t0, n), w=xk(d, t0, n))
                    if g == 10 and tail_cb is not None:
                        tail_cb(d)

    def gla():
        cv = Carver()
        qt_ = cv.take([128, 2, T], BF16)
        kt_ = cv.take([128, 2, T], BF16)
        khat = cv.take([128, NCH, 256], BF16)
        v = cv.take([128, NCH, 512], BF16)
        ogT = cv.take([128, 4, T], BF16)
        S_bf = cv.take([128, 2, 512], BF16)
        AT = [cv.take([128, 128], BF16) for _ in range(2)]
        S = cv.take([128, 2, 512], F32)
        sp_ = [cv.take([128, 256], F32) for _ in range(2)]
        eb = [cv.take([128, 2, 128], F32) for _ in range(2)]
        ed = [cv.take([128, 256], F32) for _ in range(2)]
        emb = [cv.take([128, 2, 128], F32) for _ in range(2)]
        dec = cv.take([128, 2, 16], F32)
        junk = [cv.take([128, 512], F32) for _ in range(2)]
        ss = cv.take([128, 4], F32)
        G = "gla"
        P.add("sp", lambda e: e.dma_start(out=wlra[0:17, 0:1024], in_=wlra_d), w=["rstd"], dsem="c4")
        P.add("sp", lambda e: e.dma_start(out=hnb[:, 0:512], in_=hnb_d), w=["sqt"], dsem="c5")
        P.add("dve", lambda e: e.memset(lr_aug[0:32, :], 1.0), w=["sqacc"])
        for ti, (t0, n) in enumerate(TT):
            b = next_bank()
            for kc in range(KC):
                P.add("pe", lambda e, b=b, n=n, t0=t0, kc=kc: e.matmul(pb[b][0:16, 0:n], lhsT=wlr[:, kc * 16:(kc + 1) * 16], rhs=h[:, kc, t0:t0 + n], start=(kc == 0), stop=(kc == KC - 1)),
                      r=["wlr", ("h", kc)], w=[("pb", b)])
            P.add("act", lambda e, b=b, n=n, t0=t0: e.activation(out=lr_aug[0:16, t0:t0 + n], in_=pb[b][0:16, 0:n], func=AF.Copy),
                  r=[("pb", b)], w=["sqacc"])

        def head(hd):
            (wq, kq), (wk, kk), (wv0, kv0), (wv1, kv1) = acquire(4)
            wqv = wq[:].rearrange("p (k m) -> p k m", k=KC)
            wkv = wk[:].rearrange("p (k m) -> p k m", k=KC)
            wvv = [wv0[:].rearrange("p (k m) -> p k m", k=KC), wv1[:].rearrange("p (k m) -> p k m", k=KC)]
            kvk = [kv0, kv1]
            P.add("sp", lambda e, hd=hd: e.dma_start(out=S[:], in_=sout_d[hd].rearrange("k p v -> p k v")), r=[("sout", hd)], w=["S"], dsem="c6", ucls=G)
            P.add("act", lambda e: e.activation(out=S_bf[:], in_=S[:], func=AF.Copy), r=["S"], w=["Sbf"], ucls=G)

            def inproj(c):
                n = 128 if c < 8 else 16
                t0 = c * 128
                pz, pbT, pd, pq, pk, pkt, pv = 0, 1, 0, 2, 3, 4, 5
                i2 = c % 2
                P.add("pe", lambda e: e.matmul(pb[pz][0:n, 0:256], lhsT=lr_aug[0:17, t0:t0 + n], rhs=wlra[0:17, hd * 256:(hd + 1) * 256], start=True, stop=True),
                      r=["sqacc", "rstd"], w=[("pb", pz)])
                P.add("act", lambda e: e.activation(out=sp_[i2][0:n, :], in_=pb[pz][0:n, 0:256], func=AF.Exp, scale=-1.0), r=[("pb", pz)], w=[("sp", i2)], ucls=G)
                P.add("act", lambda e: e.activation(out=sp_[i2][0:n, :], in_=sp_[i2][0:n, :], func=AF.Ln, bias=oneb[0:n, 0:1]), r=[("sp", i2), "oneb"], w=[("sp", i2)], ucls=G)
                pbTv = pb[pbT][:, 0:256].rearrange("p (k t) -> p k t", k=2)
                pqv = pb[pq][:, 0:256].rearrange("p (k t) -> p k t", k=2)
                pkv = pb[pk][:, 0:256].rearrange("p (k t) -> p k t", k=2)
                for (pv_, wv_, wkey, bk) in ((pqv, wqv, kq, pq), (pkv, wkv, kk, pk)):
                    for fc in range(2):
                        for kc in range(KC):
                            P.add("pe", lambda e, pv_=pv_, wv_=wv_, fc=fc, kc=kc: e.matmul(pv_[:, fc, 0:n], lhsT=wv_[:, kc, fc * 128:(fc + 1) * 128], rhs=h[:, kc, t0:t0 + n], start=(kc == 0), stop=(kc == KC - 1)),
                                  r=[wkey, ("h", kc)], w=[("pb", bk)])
                for kc in range(KC):
                    P.add("pe", lambda e, kc=kc: e.matmul(pb[pkt][0:n, 0:256], lhsT=h[:, kc, t0:t0 + n], rhs=wkv[:, kc, :], start=(kc == 0), stop=(kc == KC - 1)),
                          r=[kk, ("h", kc)], w=[("pb", pkt)])
                for hf in range(2):
                    for kc in range(KC):
                        P.add("pe", lambda e, hf=hf, kc=kc: e.matmul(pb[pv][0:n, hf * 256:(hf + 1) * 256], lhsT=h[:, kc, t0:t0 + n], rhs=wvv[hf][:, kc, :], start=(kc == 0), stop=(kc == KC - 1)),
                              r=[kvk[hf], ("h", kc)], w=[("pb", pv)])
                P.add("act", lambda e: e.activation(out=v[0:n, c, :], in_=pb[pv][0:n, :], func=AF.Copy), r=[("pb", pv)], w=[("v", c)], ucls=G)
                for kc in range(2):
                    P.add("pe", lambda e, kc=kc: e.matmul(pbTv[:, kc, 0:n], lhsT=sp_[i2][0:n, kc * 128:(kc + 1) * 128], rhs=uneg[0:n, 0:n], start=True, stop=True),
                          r=[("sp", i2), "cst"], w=[("pb", pbT)], ucls=G)
                P.add("pe", lambda e: e.matmul(pb[pd][0:n, 0:256], lhsT=lneg[0:n, 0:n], rhs=sp_[i2][0:n, :], start=True, stop=True),
                      r=[("sp", i2), "cst"], w=[("pb", pd)], ucls=G)
                P.add("act", lambda e: e.activation(out=eb[i2][:, :, 0:n], in_=pbTv[:, :, 0:n], func=AF.Exp), r=[("pb", pbT)], w=[("eb", i2)], ucls=G)
                P.add("act", lambda e: e.activation(out=emb[i2][:, :, 0:n], in_=pbTv[:, :, 0:n], func=AF.Exp, scale=-1.0), r=[("pb", pbT)], w=[("emb", i2)], ucls=G)
                P.add("act", lambda e: e.activation(out=dec[:, :, c:c + 1], in_=pbTv[:, :, n - 1:n], func=AF.Exp), r=[("pb", pbT)], w=[("dec", c)], ucls=G)
                P.add("act", lambda e: e.activation(out=ed[i2][0:n, :], in_=pb[pd][0:n, 0:256], func=AF.Exp), r=[("pb", pd)], w=[("ed", i2)], ucls=G)
                P.add("dve", lambda e: e.scalar_tensor_tensor(out=qt_[:, :, t0:t0 + n], in0=pqv[:, :, 0:n], scalar=1.0 / 16.0, in1=eb[i2][:, :, 0:n], op0=ALU.mult, op1=ALU.mult),
                      r=[("pb", pq), ("eb", i2)], w=[("qt", c)], ucls=G)
                P.add("dve", lambda e: e.tensor_tensor(out=kt_[:, :, t0:t0 + n], in0=pkv[:, :, 0:n], in1=emb[i2][:, :, 0:n], op=ALU.mult),
                      r=[("pb", pk), ("emb", i2)], w=[("kt", c)], ucls=G)
                P.add("dve", lambda e: e.tensor_tensor(out=khat[0:n, c, :], in0=pb[pkt][0:n, 0:256], in1=ed[i2][0:n, :], op=ALU.mult),
                      r=[("pb", pkt), ("ed", i2)], w=[("khat", c)], ucls=G)

            def recur(c):
                n = 128 if c < 8 else 16
                t0 = c * 128
                i2 = c % 2
                pA, po, pS0, pS1 = 6, 7, 5, 0
                for kc in range(2):
                    P.add("pe", lambda e, kc=kc: e.matmul(pb[pA][0:n, 0:n], lhsT=kt_[:, kc, t0:t0 + n], rhs=qt_[:, kc, t0:t0 + n], start=(kc == 0), stop=(kc == 1)),
                          r=[("kt", c), ("qt", c)], w=[("pb", pA)], ucls=G)
                P.add("dve", lambda e: e.tensor_tensor(out=AT[i2][0:n, 0:n], in0=pb[pA][0:n, 0:n], in1=mask[0:n, 0:n], op=ALU.mult),
                      r=[("pb", pA), "cst"], w=[("AT", i2)], ucls=G)
                if c < 8:
                    for kc, pS in ((0, pS0), (1, pS1)):
                        P.add("pe", lambda e, kc=kc, pS=pS: e.matmul(pb[pS][:, :], lhsT=khat[0:n, c, kc * 128:(kc + 1) * 128], rhs=v[0:n, c, :], start=True, stop=True),
                              r=[("khat", c), ("v", c)], w=[("pb", pS)], ucls=G)
                for kc in range(2):
                    P.add("pe", lambda e, kc=kc: e.matmul(pb[po][0:n, :], lhsT=qt_[:, kc, t0:t0 + n], rhs=S_bf[:, kc, :], start=(kc == 0), stop=False),
                          r=[("qt", c), "Sbf"], w=[("pb", po)], ucls=G)
                P.add("pe", lambda e: e.matmul(pb[po][0:n, :], lhsT=AT[i2][0:n, 0:n], rhs=v[0:n, c, :], start=False, stop=True),
                      r=[("AT", i2), ("v", c)], w=[("pb", po)], ucls=G)
                if c < 8:
                    for kc, pS in ((0, pS0), (1, pS1)):
                        P.add("dve", lambda e, kc=kc, pS=pS: e.scalar_tensor_tensor(out=S[:, kc, :], in0=S[:, kc, :], scalar=dec[:, kc, c:c + 1], in1=pb[pS][:, :], op0=ALU.mult, op1=ALU.add),
                              r=[("pb", pS), "S", ("dec", c)], w=["S"], ucls=G)
                    P.add("act", lambda e: e.activation(out=S_bf[:], in_=S[:], func=AF.Copy), r=["S"], w=["Sbf"], ucls=G)
                P.add("act", lambda e: e.activation(out=junk[i2][0:n, :], in_=pb[po][0:n, :], func=AF.Square), r=[("pb", po)], w=[("junk", i2)], ucls=G)
                P.add("dve", lambda e: e.reduce_sum(out=ss[0:n, i2:i2 + 1], in_=junk[i2][0:n, :], axis=AX.X), r=[("junk", i2)], w=[("ss", i2)], ucls=G)
                P.add("act", lambda e: e.activation(out=ss[0:n, i2:i2 + 1], in_=ss[0:n, i2:i2 + 1], func=AF.Ln, bias=epsb[0:n, 0:1], scale=1.0 / 512.0), r=[("ss", i2), "eps"], w=[("ss", i2)], ucls=G)
                P.add("act", lambda e: e.activation(out=ss[0:n, i2:i2 + 1], in_=ss[0:n, i2:i2 + 1], func=AF.Exp, scale=-0.5), r=[("ss", i2)], w=[("ss", i2)], ucls=G)
                P.add("dve", lambda e: e.scalar_tensor_tensor(out=v[0:n, c, :], in0=pb[po][0:n, :], scalar=ss[0:n, i2:i2 + 1], in1=hnb[0:n, 0:512], op0=ALU.mult, op1=ALU.mult),
                      r=[("pb", po), ("ss", i2), "sqt"], w=[("v", c)], ucls=G)

            for c in range(NCH):
                inproj(c)
                if c >= 1:
                    recur(c - 1)
            recur(NCH - 1)

            (wr0, kr0), (wr1, kr1) = acquire(2)
            wrv = [wr0[:].rearrange("p (k m) -> p k m", k=KC), wr1[:].rearrange("p (k m) -> p k m", k=KC)]
            krk = [kr0, kr1]
            ptv = pbt[:, 0:512].rearrange("p (k t) -> p k t", k=4)
            def rmm(c):
                n = 128 if c < 8 else 16
                t0 = c * 128
                i2 = c % 2
                pr = 4 + i2
                for hf in range(2):
                    for kc in range(KC):
                        P.add("pe", lambda e, hf=hf, kc=kc: e.matmul(pb[pr][0:n, hf * 256:(hf + 1) * 256], lhsT=h[:, kc, t0:t0 + n], rhs=wrv[hf][:, kc, :], start=(kc == 0), stop=(kc == KC - 1)),
                              r=[krk[hf], ("h", kc)], w=[("pb", pr)])
                P.add("act", lambda e: e.activation(out=junk[i2][0:n, :], in_=pb[pr][0:n, :], func=AF.Silu), r=[("pb", pr)], w=[("junk", i2)], ucls=G)
                P.add("dve", lambda e: e.tensor_tensor(out=v[0:n, c, :], in0=v[0:n, c, :], in1=junk[i2][0:n, :], op=ALU.mult), r=[("v", c), ("junk", i2)], w=[("v", c)], ucls=G)

            def rtr(c):
                n = 128 if c < 8 else 16
                t0 = c * 128
                for vc in range(4):
                    P.add("pe", lambda e, vc=vc: e.transpose(out=ptv[:, vc, 0:n], in_=v[0:n, c, vc * 128:(vc + 1) * 128], identity=ident[0:n, 0:n]),
                          r=[("v", c), "ident"], w=[("pb", 7)], ucls=G)
                P.add("act", lambda e: e.activation(out=ogT[:, :, t0:t0 + n], in_=ptv[:, :, 0:n], func=AF.Copy), r=[("pb", 7)], w=[("ogT", c)], ucls=G)

            for c in range(NCH):
                rmm(c)
                if c >= 1:
                    rtr(c - 1)
            rtr(NCH - 1)
            for dh in range(2):
                (wo, ko), = acquire(1)
                wov = wo[:].rearrange("p (f m) -> p f m", f=4)
                for dd in range(8):
                    d = dh * 8 + dd
                    for ti, (t0, n) in enumerate(TT):
                        b = next_bank()
                        cs_ = [c for c in range(NCH) if c * 128 < t0 + n and (c + 1) * 128 > t0]
                        for vc in range(4):
                            P.add("pe", lambda e, b=b, n=n, t0=t0, vc=vc, dd=dd, wov=wov: e.matmul(pb[b][:, 0:n], lhsT=wov[:, vc, dd * 128:(dd + 1) * 128], rhs=ogT[:, vc, t0:t0 + n], start=(vc == 0), stop=(vc == 3)),
                                  r=[ko] + [("ogT", c) for c in cs_], w=[("pb", b)], ucls=G)
                        P.add("dve", lambda e, b=b, n=n, t0=t0, d=d: e.tensor_tensor(out=x[:, d, t0:t0 + n], in0=x[:, d, t0:t0 + n], in1=pb[b][:, 0:n], op=ALU.add),
                              r=[("pb", b), ("x", d, ti)], w=[("x", d, ti)])
                    if hd == 3:
                        stats_chunk(d)

        for hd in range(4):
            head(hd)

    def gla_prefix():
        cv = Carver()
        khat2 = cv.take([128, 2, 256], BF16)
        v2 = cv.take([128, 2, 512], BF16)
        S = cv.take([128, 2, 512], F32)
        sp_ = [cv.take([128, 256], F32) for _ in range(2)]
        ed = [cv.take([128, 256], F32) for _ in range(2)]
        dec = cv.take([128, 2, 16], F32)
        G = "glap"
        P.add("sp", lambda e: e.dma_start(out=wlra[0:17, 0:1024], in_=wlra_d), w=["rstd"], dsem="c7")
        P.add("dve", lambda e: e.memset(lr_aug[0:32, :], 1.0), w=["sqacc"])
        for ti, (t0, n) in enumerate(TT):
            b = next_bank()
            for kc in range(KC):
                P.add("pe", lambda e, b=b, n=n, t0=t0, kc=kc: e.matmul(pb[b][0:16, 0:n], lhsT=wlr[:, kc * 16:(kc + 1) * 16], rhs=h[:, kc, t0:t0 + n], start=(kc == 0), stop=(kc == KC - 1)),
                      r=["wlr", ("h", kc)], w=[("pb", b)])
            P.add("act", lambda e, b=b, n=n, t0=t0: e.activation(out=lr_aug[0:16, t0:t0 + n], in_=pb[b][0:16, 0:n], func=AF.Copy),
                  r=[("pb", b)], w=["sqacc"])

        def phead(hd):
            (wk, kk), (wv0, kv0), (wv1, kv1) = acquire(3)
            wkv = wk[:].rearrange("p (k m) -> p k m", k=KC)
            wvv = [wv0[:].rearrange("p (k m) -> p k m", k=KC), wv1[:].rearrange("p (k m) -> p k m", k=KC)]
            kvk = [kv0, kv1]
            P.add("dve", lambda e: e.memset(S[:], 0.0), w=["S"], ucls=G)

            def pchunk(c):
                n = 128
                t0 = c * 128
                pz, pbT, pd, pkt, pv = 0, 1, 2, 5, 6
                i2 = c % 2
                P.add("pe", lambda e: e.matmul(pb[pz][0:n, 0:256], lhsT=lr_aug[0:17, t0:t0 + n], rhs=wlra[0:17, hd * 256:(hd + 1) * 256], start=True, stop=True),
                      r=["sqacc", "rstd"], w=[("pb", pz)])
                P.add("act", lambda e: e.activation(out=sp_[i2][0:n, :], in_=pb[pz][0:n, 0:256], func=AF.Exp, scale=-1.0), r=[("pb", pz)], w=[("sp", i2)], ucls=G)
                P.add("act", lambda e: e.activation(out=sp_[i2][0:n, :], in_=sp_[i2][0:n, :], func=AF.Ln, bias=oneb[0:n, 0:1]), r=[("sp", i2), "oneb"], w=[("sp", i2)], ucls=G)
                pbTv = pb[pbT][:, 0:256].rearrange("p (k t) -> p k t", k=2)
                for kc in range(2):
                    P.add("pe", lambda e, kc=kc: e.matmul(pbTv[:, kc, 0:n], lhsT=sp_[i2][0:n, kc * 128:(kc + 1) * 128], rhs=uneg[0:n, 0:n], start=True, stop=True),
                          r=[("sp", i2), "cst"], w=[("pb", pbT)], ucls=G)
                P.add("pe", lambda e: e.matmul(pb[pd][0:n, 0:256], lhsT=lneg[0:n, 0:n], rhs=sp_[i2][0:n, :], start=True, stop=True),
                      r=[("sp", i2), "cst"], w=[("pb", pd)], ucls=G)
                P.add("act", lambda e: e.activation(out=dec[:, :, c:c + 1], in_=pbTv[:, :, n - 1:n], func=AF.Exp), r=[("pb", pbT)], w=[("dec", c)], ucls=G)
                P.add("act", lambda e: e.activation(out=ed[i2][0:n, :], in_=pb[pd][0:n, 0:256], func=AF.Exp), r=[("pb", pd)], w=[("ed", i2)], ucls=G)
                for kc in range(KC):
                    P.add("pe", lambda e, kc=kc: e.matmul(pb[pkt][0:n, 0:256], lhsT=h[:, kc, t0:t0 + n], rhs=wkv[:, kc, :], start=(kc == 0), stop=(kc == KC - 1)),
                          r=[kk, ("h", kc)], w=[("pb", pkt)])
                P.add("dve", lambda e: e.tensor_tensor(out=khat2[0:n, i2, :], in0=pb[pkt][0:n, 0:256], in1=ed[i2][0:n, :], op=ALU.mult),
                      r=[("pb", pkt), ("ed", i2)], w=[("khat", i2)], ucls=G)
                for hf in range(2):
                    for kc in range(KC):
                        P.add("pe", lambda e, hf=hf, kc=kc: e.matmul(pb[pv][0:n, hf * 256:(hf + 1) * 256], lhsT=h[:, kc, t0:t0 + n], rhs=wvv[hf][:, kc, :], start=(kc == 0), stop=(kc == KC - 1)),
                              r=[kvk[hf], ("h", kc)], w=[("pb", pv)])
                P.add("act", lambda e: e.activation(out=v2[0:n, i2, :], in_=pb[pv][0:n, :], func=AF.Copy), r=[("pb", pv)], w=[("v", i2)], ucls=G)

            def pupd(c):
                n = 128
                i2 = c % 2
                for kc, pS in ((0, 3), (1, 4)):
                    P.add("pe", lambda e, kc=kc, pS=pS: e.matmul(pb[pS][:, :], lhsT=khat2[0:n, i2, kc * 128:(kc + 1) * 128], rhs=v2[0:n, i2, :], start=True, stop=True),
                          r=[("khat", i2), ("v", i2)], w=[("pb", pS)], ucls=G)
                    P.add("dve", lambda e, kc=kc, pS=pS: e.scalar_tensor_tensor(out=S[:, kc, :], in0=S[:, kc, :], scalar=dec[:, kc, c:c + 1], in1=pb[pS][:, :], op0=ALU.mult, op1=ALU.add),
                          r=[("pb", pS), "S", ("dec", c)], w=["S"], ucls=G)

            for c in range(8):
                pchunk(c)
                if c >= 1:
                    pupd(c - 1)
            pupd(7)
            P.add("sp", lambda e: e.dma_start(out=sout_d[hd].rearrange("k p v -> p k v"), in_=S[:]), r=["S"], w=[("sout", hd)], dsem="so", ucls=G)

        for hd in range(4):
            phead(hd)

    def pool_mixer(inorm):
        cv = Carver()
        pooled = cv.take([128, KC, T], BF16)
        hp = cv.take([128, T], F32)
        sa = cv.take([128, T], F32)
        sb_ = cv.take([128, T], F32)
        Q = "pool"
        rms_stats(TT, pre_done=True)
        for c in range(KC):
            win = WINS[c // 4]
            P.add("dve", lambda e, c=c: e.scalar_tensor_tensor(out=hp[:], in0=x[:, c, :], scalar=gn[:, inorm * 16 + c:inorm * 16 + c + 1], in1=rstd[:], op0=ALU.mult, op1=ALU.mult),
                  r=XK(c) + ["gn", "rstd"], w=["hp"], ucls=Q)
            src, skey = hp, "hp"
            bufs = [(sa, "sa"), (sb_, "sb")]
            sh = 1
            bi = 0
            while sh < win:
                dst, dkey = bufs[bi % 2]
                bi += 1
                P.add("dve", lambda e, src=src, dst=dst, sh=sh: e.tensor_tensor(out=dst[:, sh:T], in0=src[:, sh:T], in1=src[:, 0:T - sh], op=ALU.add),
                      r=[skey], w=[dkey], ucls=Q)
                P.add("dve", lambda e, src=src, dst=dst, sh=sh: e.tensor_copy(out=dst[:, 0:sh], in_=src[:, 0:sh]), r=[skey], w=[dkey], ucls=Q)
                src, skey = dst, dkey
                sh *= 2
            P.add("dve", lambda e, c=c, src=src, win=win: e.scalar_tensor_tensor(out=pooled[:, c, :], in0=src[:], scalar=1.0 / win, in1=hp[:], op0=ALU.mult, op1=ALU.subtract),
                  r=[skey, "hp"], w=[("pooled", c)], ucls=Q)
        for s in range(2):
            (wp, kp), = acquire(1)
            wpv = wp[:].rearrange("p (g k m) -> p g k m", g=2, k=4)
            for gi in range(2):
                g = 2 * s + gi
                for dc in range(4):
                    d = g * 4 + dc
                    for ti, (t0, n) in enumerate(TT):
                        b = next_bank()
                        for kc in range(4):
                            P.add("pe", lambda e, b=b, n=n, t0=t0, gi=gi, kc=kc, dc=dc, g=g, wpv=wpv: e.matmul(pb[b][:, 0:n], lhsT=wpv[:, gi, kc, dc * 128:(dc + 1) * 128], rhs=pooled[:, g * 4 + kc, t0:t0 + n], start=(kc == 0), stop=(kc == 3)),
                                  r=[kp, ("pooled", g * 4 + kc)], w=[("pb", b)], ucls=Q)
                        P.add("dve", lambda e, b=b, n=n, d=d: e.tensor_scalar(out=sa[:, 0:n], in0=pb[b][:, 0:n], scalar1=pbs[:, d:d + 1], scalar2=pbs[:, 16 + d:16 + d + 1], op0=ALU.add, op1=ALU.mult),
                              r=[("pb", b), "pbs"], w=["sa"], ucls=Q)
                        P.add("dve", lambda e, n=n, t0=t0, d=d: e.tensor_tensor(out=x[:, d, t0:t0 + n], in0=x[:, d, t0:t0 + n], in1=sa[:, 0:n], op=ALU.add),
                              r=["sa", ("x", d, ti)], w=[("x", d, ti)], ucls=Q)
                    stats_chunk(d)

    def final():
        rms_stats(TTM, pre_done=True)
        for c in range(KC):
            eng = "dve"
            P.add(eng, lambda e, c=c: e.scalar_tensor_tensor(out=x[:, c, 16:T], in0=x[:, c, 16:T], scalar=gn[:, 6 * 16 + c:6 * 16 + c + 1], in1=rstd[:, 16:T], op0=ALU.mult, op1=ALU.mult),
                  r=XK(c) + ["gn", "rstd"], w=XK(c))
        dump()

    def dump():
        for c4 in range(4):
            P.add("sp", lambda e, c4=c4: e.dma_start(out=yout[512 * c4:512 * c4 + 512, :].rearrange("(c p) t -> p c t", p=128), in_=x[:, 4 * c4:4 * c4 + 4, 16:T]),
                  r=[k for i in range(4) for k in XK(4 * c4 + i)], w=[("yout", c4)], dsem="y%d" % c4)
        P.add("sp", None, r=[("yout", c4) for c4 in range(4)] + [("sout", hd) for hd in range(4)])

    TA = TT[:2]
    rmsnorm(0, TA)
    ffn(TA, stats_chunk)
    rmsnorm(1, TA, pre_done=True)
    gla_prefix()
    load_x(xin)
    stages = [
        ("n0", lambda: rmsnorm(0)), ("f0", lambda: ffn(TT, stats_chunk)),
        ("n1", lambda: rmsnorm(1, TT, True)), ("gla", gla),
        ("n2", lambda: rmsnorm(2, TT, True)), ("f1", lambda: ffn(TT, stats_chunk)),
        ("n3", lambda: rmsnorm(3, TT, True)), ("f2", lambda: ffn(TT, stats_chunk)),
        ("pool", lambda: pool_mixer(4)),
        ("n5", lambda: rmsnorm(5, TTM, True)), ("f3", lambda: ffn(TTM, stats_chunk)),
    ]
    done = False
    for name, fn in stages:
        fn()
        if stop_after == name:
            dump()
            done = True
            break
    if not done:
        final()
    P.emit()
    P.close()
    return nc


_NC_CACHE = {}


def _consts():
    j = np.arange(128)[:, None]
    i = np.arange(128)[None, :]
    ones = np.full((128, 128), 1.0 / D, np.float32)
    uneg = np.where(j <= i, -1.0 / 16.0, 0.0).astype(np.float32)
    lneg = np.where(j > i, -1.0 / 16.0, 0.0).astype(np.float32)
    mask = np.where(j <= i, 1.0, 0.0).astype(np.float32)
    ident = np.eye(128, dtype=np.float32)
    return np.ascontiguousarray(np.concatenate([ones, uneg, lneg, mask, ident], axis=1))


def prepare(x, meta, ffn_norm, ffn_w_gate, ffn_w_up, ffn_w_down, gla_norm, gla_w_in, gla_w_lr, gla_b_lr,
            gla_head_norm, gla_w_out, pool_norm, pool_w, pool_b, pool_scale, final_norm):
    f = lambda a: np.asarray(a, dtype=np.float32)
    x = f(x)
    meta = f(meta)
    slabs = []
    slabs += ffn_slabs(f(ffn_w_gate[0, 0]), f(ffn_w_up[0, 0]), f(ffn_w_down[0, 0]))
    slabs += gla_slabs(f(gla_w_in[0]), f(gla_w_out[0]))
    slabs += ffn_slabs(f(ffn_w_gate[0, 1]), f(ffn_w_up[0, 1]), f(ffn_w_down[0, 1]))
    slabs += ffn_slabs(f(ffn_w_gate[1, 0]), f(ffn_w_up[1, 0]), f(ffn_w_down[1, 0]))
    slabs += pool_slabs(f(pool_w[0]))
    slabs += ffn_slabs(f(ffn_w_gate[1, 1]), f(ffn_w_up[1, 1]), f(ffn_w_down[1, 1]))
    WS = np.ascontiguousarray(np.stack(slabs, axis=0))
    assert WS.shape[0] == N_SLABS
    gains = np.stack([f(ffn_norm[0, 0]), f(gla_norm[0]), f(ffn_norm[0, 1]), f(ffn_norm[1, 0]), f(pool_norm[0]), f(ffn_norm[1, 1]), f(final_norm)], axis=0)
    gn = np.ascontiguousarray(gains.reshape(7, KC, 128).transpose(2, 0, 1).reshape(128, 7 * 16))
    wlr = np.ascontiguousarray(f(gla_w_in[0])[:, 4096:4112].reshape(KC, 128, 16).transpose(1, 0, 2).reshape(128, 256))
    wlra = np.ascontiguousarray(np.concatenate([f(gla_w_lr[0]), f(gla_b_lr[0])[None, :]], axis=0))
    hnb = np.ascontiguousarray(np.broadcast_to(f(gla_head_norm[0])[None, :], (128, 512)))
    pbv = f(pool_b[0]).reshape(D).reshape(KC, 128).T
    psv = f(pool_scale[0]).reshape(KC, 128).T
    pbs = np.ascontiguousarray(np.concatenate([pbv, psv], axis=1))
    common = dict(WS=WS, gn=gn, cst=_consts(), wlr=wlr, wlra=wlra, hnb=hnb, pbs=pbs)
    xins = []
    xpres = []
    zpre = np.zeros((D, T), np.float32)
    for b in range(4):
        seq0 = np.concatenate([meta, x[b, 0:1024]], axis=0)
        seq1 = x[b, 1008:2048]
        xins.append(np.ascontiguousarray(seq0.T))
        xins.append(np.ascontiguousarray(seq1.T))
        pre1 = np.zeros((D, T), np.float32)
        pre1[:, 0:1024] = np.concatenate([meta, x[b, 0:1008]], axis=0).T
        xpres.append(zpre)
        xpres.append(pre1)
    return common, xins, xpres


def run(inputs, stop_after=None, trace=False):
    common, xins, xpres = prepare(**inputs)
    if stop_after not in _NC_CACHE:
        _NC_CACHE[stop_after] = build(stop_after)
    nc = _NC_CACHE[stop_after]
    in_maps = [dict(common, xin=xins[c], xpre=xpres[c]) for c in range(8)]
    res = run_bass_kernel_spmd(nc, in_maps, core_ids=list(range(8)), trace=trace)
    out = np.empty((4, 2048, D), np.float32)
    for c in range(8):
        b, hf = c // 2, c % 2
        out[b, hf * 1024:(hf + 1) * 1024, :] = res.results[c]["yout"].T
    return out, (res, res)


def kernel(**inputs):
    out, _ = run(inputs)
    return out
```

```python
import contextlib
import numpy as np
import concourse.bass as bass
import concourse.mybir as mybir
from concourse.bass_utils import run_bass_kernel_spmd

F32 = mybir.dt.float32
BF16 = mybir.dt.bfloat16
AF = mybir.ActivationFunctionType
ALU = mybir.AluOpType
AX = mybir.AxisListType

D = 2048
KC = 16
FF = 5632
T = 1040
TT = [(0, 512), (512, 512), (1024, 16)]
NCH = 9
NS = 5
SLAB = 4096
EPS = 1e-6
WINS = (2, 4, 8, 16)

ENGS = ("pe", "act", "dve", "pool", "sp")
BLOCK_ATTR = {"pe": "tensor", "act": "scalar", "dve": "vector", "pool": "gpsimd", "sp": "sync"}


class Prog:
    def __init__(self, nc):
        self.nc = nc
        self.ops = {e: [] for e in ENGS}
        self.lastw = {}
        self.rd = {}
        self.dcount = {}
        self.stack = contextlib.ExitStack()
        self.dsems = {}
        self.ucur = None
        self.uops = ({}, set())
        self.ubar = ({}, set())

    def sb(self, name, shape, dt):
        return self.stack.enter_context(self.nc.sbuf_tensor(name, list(shape), dt))

    def ps(self, name, shape, dt):
        return self.stack.enter_context(self.nc.psum_tensor(name, list(shape), dt))

    def add(self, eng, fn, r=(), w=(), dsem=None, ucls=None):
        idx = len(self.ops[eng])
        me = (eng, idx)
        deps = {}
        ddeps = set()

        def need(x):
            if x is None or x == me:
                return
            e, i = x
            o = self.ops[e][i]
            if o["dsem"] is not None:
                ddeps.add((o["dsem"], o["dcnt"]))
            else:
                if e == "pe" and eng == "pe":
                    return
                if deps.get(e, -1) < i:
                    deps[e] = i

        for k in r:
            need(self.lastw.get(k))
        for k in w:
            need(self.lastw.get(k))
            for x in self.rd.get(k, ()):
                need(x)
        if ucls is not None:
            if ucls != self.ucur:
                self.ubar = (dict(self.uops[0]), set(self.uops[1]))
                self.uops = ({}, set())
                self.ucur = ucls
            for e, i in self.ubar[0].items():
                if not (e == "pe" and eng == "pe") and deps.get(e, -1) < i:
                    deps[e] = i
            ddeps |= self.ubar[1]
        dcnt = None
        if dsem is not None:
            dcnt = self.dcount.get(dsem, 0) + 1
            self.dcount[dsem] = dcnt
        self.ops[eng].append(dict(fn=fn, deps=deps, ddeps=ddeps, dsem=dsem, dcnt=dcnt, sig=False))
        if ucls is not None:
            if dsem is not None:
                self.uops[1].add((dsem, dcnt))
            else:
                self.uops[0][eng] = idx
        for k in r:
            self.rd.setdefault(k, []).append(me)
        for k in w:
            self.lastw[k] = me
            self.rd[k] = []
        return me

    def emit(self):
        nc = self.nc
        for e in ENGS:
            for o in self.ops[e]:
                for (de, di) in o["deps"].items():
                    self.ops[de][di]["sig"] = True
        cnt = {}
        for e in ENGS:
            c = 0
            arr = []
            for o in self.ops[e]:
                if o["sig"]:
                    c += 1
                arr.append(c)
            cnt[e] = arr
        esem = {e: self.stack.enter_context(nc.semaphore("es_" + e)) for e in ENGS}
        for name in self.dcount:
            self.dsems[name] = self.stack.enter_context(nc.semaphore("ds_" + name))
        block = self.stack.enter_context(nc.Block())

        def make(e):
            def body(engine):
                waited = {}
                for o in self.ops[e]:
                    for (de, di) in o["deps"].items():
                        v = cnt[de][di]
                        key = "e" + de
                        if waited.get(key, 0) < v:
                            engine.wait_ge(esem[de], v)
                            waited[key] = v
                    for (ds, dc) in o["ddeps"]:
                        key = "d" + ds
                        v = 16 * dc
                        if waited.get(key, 0) < v:
                            engine.wait_ge(self.dsems[ds], v)
                            waited[key] = v
                    if o["fn"] is None:
                        continue
                    ins = o["fn"](engine)
                    if o["dsem"] is not None:
                        ins.then_inc(self.dsems[o["dsem"]], 16)
                    elif o["sig"]:
                        ins.then_inc(esem[e], 1)
            return body

        for e in ENGS:
            if self.ops[e]:
                getattr(block, BLOCK_ATTR[e])(make(e))

    def close(self):
        self.stack.close()


def ffn_slabs(wg, wu, wd):
    g = wg.reshape(KC, 128, 44, 128).transpose(2, 1, 0, 3)
    u = wu.reshape(KC, 128, 44, 128).transpose(2, 1, 0, 3)
    gu = np.stack([g, u], axis=2).reshape(44, 128, SLAB)
    dn = wd.reshape(11, 4, 128, 2, 1024).transpose(0, 3, 2, 1, 4).reshape(11, 2, 128, SLAB)
    out = []
    for gi in range(11):
        for fi in range(4):
            out.append(gu[4 * gi + fi])
        out.append(dn[gi, 0])
        out.append(dn[gi, 1])
    return out


def gla_slabs(w_in, w_out):
    out = []

    def cols(c0):
        return w_in[:, c0:c0 + 256].reshape(KC, 128, 256).transpose(1, 0, 2).reshape(128, SLAB)

    for hd in range(4):
        out.append(cols(hd * 256))
        out.append(cols(1024 + hd * 256))
        out.append(cols(2048 + hd * 512))
        out.append(cols(2048 + hd * 512 + 256))
        out.append(cols(4112 + hd * 512))
        out.append(cols(4112 + hd * 512 + 256))
        wo = w_out[hd * 512:(hd + 1) * 512].reshape(4, 128, 2, 1024).transpose(2, 1, 0, 3).reshape(2, 128, SLAB)
        out.append(wo[0])
        out.append(wo[1])
    return out


def pool_slabs(pw):
    out = []
    for s in range(2):
        out.append(pw[2 * s:2 * s + 2].reshape(2, 4, 128, 512).transpose(2, 0, 1, 3).reshape(128, SLAB))
    return out


N_SLABS = 66 * 4 + 32 + 2


STAGE_SLABS = {"n0": 5, "f0": 66, "n1": 66, "gla": 98, "n2": 98, "f1": 164, "n3": 164, "f2": 230, "pool": 232, "n5": 232, "f3": 298, None: 298}


def build(stop_after=None):
    nc = bass.Bass("TRN2", target_bir_lowering=False)
    xin = nc.dram_tensor("xin", [D, T], F32, kind="ExternalInput").ap()
    WS = nc.dram_tensor("WS", [N_SLABS, 128, SLAB], F32, kind="ExternalInput").ap()
    gn_d = nc.dram_tensor("gn", [128, 7 * 16], F32, kind="ExternalInput").ap()
    cst_d = nc.dram_tensor("cst", [128, 5 * 128], F32, kind="ExternalInput").ap()
    wlr_d = nc.dram_tensor("wlr", [128, 16 * 16], F32, kind="ExternalInput").ap()
    wlra_d = nc.dram_tensor("wlra", [17, 1024], F32, kind="ExternalInput").ap()
    hnb_d = nc.dram_tensor("hnb", [128, 512], F32, kind="ExternalInput").ap()
    pbs_d = nc.dram_tensor("pbs", [128, 32], F32, kind="ExternalInput").ap()
    xpre = nc.dram_tensor("xpre", [D, T], F32, kind="ExternalInput").ap()
    sout_d = nc.dram_tensor("s_out", [4, 2, 128, 512], F32, kind="ExternalOutput").ap()
    yout = nc.dram_tensor("yout", [D, 1024], F32, kind="ExternalOutput").ap()

    P = Prog(nc)
    x = P.sb("x", [128, KC, T], F32)
    h = P.sb("h", [128, KC, T], BF16)
    slots = [P.sb("slot%d" % i, [128, SLAB], BF16) for i in range(NS)]
    sqt = P.sb("sqt", [128, T], F32)
    sqacc = P.sb("sqacc", [128, T], F32)
    rstd = P.sb("rstd", [128, T], F32)
    gn = P.sb("gn_sb", [128, 7 * 16], F32)
    cst = P.sb("cst_sb", [128, 5 * 128], F32)
    ident = P.sb("ident", [128, 128], BF16)
    epsb = P.sb("epsb", [128, 1], F32)
    oneb = P.sb("oneb", [128, 1], F32)
    wlr = P.sb("wlr_sb", [128, 16 * 16], BF16)
    pbs = P.sb("pbs_sb", [128, 32], F32)
    UB = 51200
    U = P.sb("U", [128, UB // 2], BF16)
    pb = [P.ps("pb%d" % i, [128, 512], F32) for i in range(8)]
    pbt = pb[7][:].bitcast(BF16)
    sqt2 = P.sb("sqt2", [128, T], F32)

    ones = cst[:, 0:128]
    uneg = cst[:, 128:256]
    lneg = cst[:, 256:384]
    mask = cst[:, 384:512]
    identf = cst[:, 512:640]
    lr_aug = sqacc
    wlra = rstd
    hnb = sqt

    class Carver:
        def __init__(self):
            self.off = 0

        def take(self, shape, dt):
            n = int(np.prod(shape[1:]))
            nb = n * (2 if dt == BF16 else 4)
            nb = (nb + 63) // 64 * 64
            assert self.off + nb <= UB, (self.off, nb)
            a = U[:, self.off // 2:self.off // 2 + (n if dt == BF16 else 2 * n)]
            if dt == F32:
                a = a.bitcast(F32)
            self.off += nb
            if len(shape) == 3:
                a = a.rearrange("p (a b) -> p a b", a=shape[1])
            return a

    st = dict(next_use=0, next_dma=0)
    SEQ = list(range(66))
    for hd_ in range(4):
        SEQ += [66 + hd_ * 8 + 1, 66 + hd_ * 8 + 2, 66 + hd_ * 8 + 3]
    SEQ += list(range(N_SLABS))

    def acquire(n):
        j0 = st["next_use"]
        lim = min(j0 + NS, len(SEQ))
        while st["next_dma"] < lim:
            j = st["next_dma"]
            s = j % NS
            P.add("pool", lambda e, j=j, s=s: e.dma_start(out=slots[s][:], in_=WS[SEQ[j]]), w=[("slot", s)], dsem="s%d" % s)
            st["next_dma"] += 1
        st["next_use"] += n
        return [(slots[(j0 + i) % NS], ("slot", (j0 + i) % NS)) for i in range(n)]

    def load_x(src):
        for c4 in range(4):
            P.add("sp", lambda e, c4=c4: e.dma_start(out=x[:, 4 * c4:4 * c4 + 4, :], in_=src[512 * c4:512 * c4 + 512, :].rearrange("(c p) t -> p c t", p=128)),
                  w=[("x", 4 * c4 + i, ti) for i in range(4) for ti in range(3)], dsem="x%d" % c4)

    load_x(xpre)
    P.add("sp", lambda e: e.dma_start(out=gn[:], in_=gn_d), w=["gn"], dsem="c0")
    P.add("sp", lambda e: e.dma_start(out=cst[:], in_=cst_d), w=["cst"], dsem="c1")
    P.add("sp", lambda e: e.dma_start(out=pbs[:], in_=pbs_d), w=["pbs"], dsem="c2")
    P.add("pool", lambda e: e.dma_start(out=wlr[:], in_=wlr_d), w=["wlr"], dsem="c3")
    P.add("dve", lambda e: e.memset(epsb[:], EPS), w=["eps"])
    P.add("dve", lambda e: e.memset(oneb[:], 1.0), w=["oneb"])
    P.add("dve", lambda e: e.tensor_copy(out=ident[:], in_=identf), r=["cst"], w=["ident"])

    bank_rr = dict(i=0)

    def next_bank():
        b = bank_rr["i"] % 7
        bank_rr["i"] += 1
        return b

    XK = lambda c: [("x", c, ti) for ti in range(3)]
    TTM = [(16, 512), (528, 512)]

    def xk(d, t0, n):
        return [("x", d, ti) for ti, (a0, an) in enumerate(TT) if a0 < t0 + n and a0 + an > t0]

    sqbufs = [sqt, sqt2]
    sqkeys = ["sqt", "sqt2"]

    def stats_chunk(c):
        if c == 0:
            P.add("act", lambda e: e.activation(out=sqacc[:], in_=x[:, 0, :], func=AF.Square), r=XK(0), w=["sqacc"])
        else:
            P.add("act", lambda e, c=c: e.activation(out=sqbufs[c % 2][:], in_=x[:, c, :], func=AF.Square), r=XK(c), w=[sqkeys[c % 2]])
            P.add("pool", lambda e, c=c: e.tensor_tensor(out=sqacc[:], in0=sqacc[:], in1=sqbufs[c % 2][:], op=ALU.add), r=["sqacc", sqkeys[c % 2]], w=["sqacc"])

    def stats_finish(tiles):
        for (t0, n) in tiles:
            b = next_bank()
            P.add("pe", lambda e, b=b, t0=t0, n=n: e.matmul(pb[b][:, 0:n], lhsT=ones, rhs=sqacc[:, t0:t0 + n], start=True, stop=True),
                  r=["cst", "sqacc"], w=[("pb", b)])
            P.add("act", lambda e, b=b, t0=t0, n=n: e.activation(out=rstd[:, t0:t0 + n], in_=pb[b][:, 0:n], func=AF.Ln, bias=epsb[:, 0:1]),
                  r=[("pb", b), "eps"], w=["rstd"])
            P.add("act", lambda e, t0=t0, n=n: e.activation(out=rstd[:, t0:t0 + n], in_=rstd[:, t0:t0 + n], func=AF.Exp, scale=-0.5),
                  r=["rstd"], w=["rstd"])

    def rms_stats(tiles, pre_done=False):
        if not pre_done:
            for c in range(KC):
                stats_chunk(c)
        stats_finish(tiles)

    def rmsnorm(i, tiles=TT, pre_done=False):
        rms_stats(tiles, pre_done)
        lo = tiles[0][0]
        hi = tiles[-1][0] + tiles[-1][1]
        for c in range(KC):
            eng = "dve"
            P.add(eng, lambda e, c=c: e.scalar_tensor_tensor(out=h[:, c, lo:hi], in0=x[:, c, lo:hi], scalar=gn[:, i * 16 + c:i * 16 + c + 1], in1=rstd[:, lo:hi], op0=ALU.mult, op1=ALU.mult),
                  r=XK(c) + ["gn", "rstd"], w=[("h", c)])

    def ffn(tiles=TT, tail_cb=None):
        cv = Carver()
        a = cv.take([128, 4, T], BF16)
        tmp = [cv.take([128, 512], F32) for _ in range(3)]
        for g in range(11):
            for fi in range(4):
                (slab, sk), = acquire(1)
                sv = slab[:].rearrange("p (w k m) -> p w k m", w=2, k=KC)
                for which in range(2):
                    for ti, (t0, n) in enumerate(tiles):
                        b = which * 3 + ti
                        for kc in range(KC):
                            P.add("pe", lambda e, b=b, n=n, t0=t0, which=which, kc=kc, sv=sv: e.matmul(pb[b][:, 0:n], lhsT=sv[:, which, kc, :], rhs=h[:, kc, t0:t0 + n], start=(kc == 0), stop=(kc == KC - 1)),
                                  r=[sk, ("h", kc)], w=[("pb", b)])
                for ti, (t0, n) in enumerate(tiles):
                    P.add("act", lambda e, ti=ti, n=n: e.activation(out=tmp[ti][:, 0:n], in_=pb[ti][:, 0:n], func=AF.Silu),
                          r=[("pb", ti)], w=[("tmp", ti)], ucls="ffn")
                    P.add("dve", lambda e, ti=ti, n=n, t0=t0, fi=fi: e.tensor_tensor(out=a[:, fi, t0:t0 + n], in0=tmp[ti][:, 0:n], in1=pb[3 + ti][:, 0:n], op=ALU.mult),
                          r=[("tmp", ti), ("pb", 3 + ti)], w=[("a", fi, ti)], ucls="ffn")
            for dh in range(2):
                (slab, sk), = acquire(1)
                sv = slab[:].rearrange("p (f m) -> p f m", f=4)
                for dd in range(8):
                    d = dh * 8 + dd
                    for ti, (t0, n) in enumerate(tiles):
                        b = next_bank()
                        for fi in range(4):
                            P.add("pe", lambda e, b=b, n=n, t0=t0, fi=fi, dd=dd, sv=sv: e.matmul(pb[b][:, 0:n], lhsT=sv[:, fi, dd * 128:(dd + 1) * 128], rhs=a[:, fi, t0:t0 + n], start=(fi == 0), stop=(fi == 3)),
                                  r=[sk, ("a", fi, ti)], w=[("pb", b)], ucls="ffn")
                        P.add("dve", lambda e, b=b, n=n, t0=t0, d=d: e.scalar_tensor_tensor(out=x[:, d, t0:t0 + n], in0=pb[b][:, 0:n], scalar=0.5, in1=x[:, d, t0:t0 + n], op0=ALU.mult, op1=ALU.add),
                              r=[("pb", b)] + xk(d, t0, n), w=xk(d, t0, n))
                    if g == 10 and tail_cb is not None:
                        tail_cb(d)

    def gla():
        cv = Carver()
        qt_ = cv.take([128, 2, T], BF16)
        kt_ = cv.take([128, 2, T], BF16)
        khat = cv.take([128, NCH, 256], BF16)
        v = cv.take([128, NCH, 512], BF16)
        ogT = cv.take([128, 4, T], BF16)
        S_bf = cv.take([128, 2, 512], BF16)
        AT = [cv.take([128, 128], BF16) for _ in range(2)]
        S = cv.take([128, 2, 512], F32)
        sp_ = [cv.take([128, 256], F32) for _ in range(2)]
        eb = [cv.take([128, 2, 128], F32) for _ in range(2)]
        ed = [cv.take([128, 256], F32) for _ in range(2)]
        emb = [cv.take([128, 2, 128], F32) for _ in range(2)]
        dec = cv.take([128, 2, 16], F32)
        junk = [cv.take([128, 512], F32) for _ in range(2)]
        ss = cv.take([128, 4], F32)
        G = "gla"
        P.add("sp", lambda e: e.dma_start(out=wlra[0:17, 0:1024], in_=wlra_d), w=["rstd"], dsem="c4")
        P.add("sp", lambda e: e.dma_start(out=hnb[:, 0:512], in_=hnb_d), w=["sqt"], dsem="c5")
        P.add("dve", lambda e: e.memset(lr_aug[0:32, :], 1.0), w=["sqacc"])
        for ti, (t0, n) in enumerate(TT):
            b = next_bank()
            for kc in range(KC):
                P.add("pe", lambda e, b=b, n=n, t0=t0, kc=kc: e.matmul(pb[b][0:16, 0:n], lhsT=wlr[:, kc * 16:(kc + 1) * 16], rhs=h[:, kc, t0:t0 + n], start=(kc == 0), stop=(kc == KC - 1)),
                      r=["wlr", ("h", kc)], w=[("pb", b)])
            P.add("act", lambda e, b=b, n=n, t0=t0: e.activation(out=lr_aug[0:16, t0:t0 + n], in_=pb[b][0:16, 0:n], func=AF.Copy),
                  r=[("pb", b)], w=["sqacc"])

        def head(hd):
            (wq, kq), (wk, kk), (wv0, kv0), (wv1, kv1) = acquire(4)
            wqv = wq[:].rearrange("p (k m) -> p k m", k=KC)
            wkv = wk[:].rearrange("p (k m) -> p k m", k=KC)
            wvv = [wv0[:].rearrange("p (k m) -> p k m", k=KC), wv1[:].rearrange("p (k m) -> p k m", k=KC)]
            kvk = [kv0, kv1]
            P.add("sp", lambda e, hd=hd: e.dma_start(out=S[:], in_=sout_d[hd].rearrange("k p v -> p k v")), r=[("sout", hd)], w=["S"], dsem="c6", ucls=G)
            P.add("act", lambda e: e.activation(out=S_bf[:], in_=S[:], func=AF.Copy), r=["S"], w=["Sbf"], ucls=G)

            def inproj(c):
                n = 128 if c < 8 else 16
                t0 = c * 128
                pz, pbT, pd, pq, pk, pkt, pv = 0, 1, 0, 2, 3, 4, 5
                i2 = c % 2
                P.add("pe", lambda e: e.matmul(pb[pz][0:n, 0:256], lhsT=lr_aug[0:17, t0:t0 + n], rhs=wlra[0:17, hd * 256:(hd + 1) * 256], start=True, stop=True),
                      r=["sqacc", "rstd"], w=[("pb", pz)])
                P.add("act", lambda e: e.activation(out=sp_[i2][0:n, :], in_=pb[pz][0:n, 0:256], func=AF.Exp, scale=-1.0), r=[("pb", pz)], w=[("sp", i2)], ucls=G)
                P.add("act", lambda e: e.activation(out=sp_[i2][0:n, :], in_=sp_[i2][0:n, :], func=AF.Ln, bias=oneb[0:n, 0:1]), r=[("sp", i2), "oneb"], w=[("sp", i2)], ucls=G)
                pbTv = pb[pbT][:, 0:256].rearrange("p (k t) -> p k t", k=2)
                pqv = pb[pq][:, 0:256].rearrange("p (k t) -> p k t", k=2)
                pkv = pb[pk][:, 0:256].rearrange("p (k t) -> p k t", k=2)
                for (pv_, wv_, wkey, bk) in ((pqv, wqv, kq, pq), (pkv, wkv, kk, pk)):
                    for fc in range(2):
                        for kc in range(KC):
                            P.add("pe", lambda e, pv_=pv_, wv_=wv_, fc=fc, kc=kc: e.matmul(pv_[:, fc, 0:n], lhsT=wv_[:, kc, fc * 128:(fc + 1) * 128], rhs=h[:, kc, t0:t0 + n], start=(kc == 0), stop=(kc == KC - 1)),
                                  r=[wkey, ("h", kc)], w=[("pb", bk)])
                for kc in range(KC):
                    P.add("pe", lambda e, kc=kc: e.matmul(pb[pkt][0:n, 0:256], lhsT=h[:, kc, t0:t0 + n], rhs=wkv[:, kc, :], start=(kc == 0), stop=(kc == KC - 1)),
                          r=[kk, ("h", kc)], w=[("pb", pkt)])
                for hf in range(2):
                    for kc in range(KC):
                        P.add("pe", lambda e, hf=hf, kc=kc: e.matmul(pb[pv][0:n, hf * 256:(hf + 1) * 256], lhsT=h[:, kc, t0:t0 + n], rhs=wvv[hf][:, kc, :], start=(kc == 0), stop=(kc == KC - 1)),
                              r=[kvk[hf], ("h", kc)], w=[("pb", pv)])
                P.add("act", lambda e: e.activation(out=v[0:n, c, :], in_=pb[pv][0:n, :], func=AF.Copy), r=[("pb", pv)], w=[("v", c)], ucls=G)
                for kc in range(2):
                    P.add("pe", lambda e, kc=kc: e.matmul(pbTv[:, kc, 0:n], lhsT=sp_[i2][0:n, kc * 128:(kc + 1) * 128], rhs=uneg[0:n, 0:n], start=True, stop=True),
                          r=[("sp", i2), "cst"], w=[("pb", pbT)], ucls=G)
                P.add("pe", lambda e: e.matmul(pb[pd][0:n, 0:256], lhsT=lneg[0:n, 0:n], rhs=sp_[i2][0:n, :], start=True, stop=True),
                      r=[("sp", i2), "cst"], w=[("pb", pd)], ucls=G)
                P.add("act", lambda e: e.activation(out=eb[i2][:, :, 0:n], in_=pbTv[:, :, 0:n], func=AF.Exp), r=[("pb", pbT)], w=[("eb", i2)], ucls=G)
                P.add("act", lambda e: e.activation(out=emb[i2][:, :, 0:n], in_=pbTv[:, :, 0:n], func=AF.Exp, scale=-1.0), r=[("pb", pbT)], w=[("emb", i2)], ucls=G)
                P.add("act", lambda e: e.activation(out=dec[:, :, c:c + 1], in_=pbTv[:, :, n - 1:n], func=AF.Exp), r=[("pb", pbT)], w=[("dec", c)], ucls=G)
                P.add("act", lambda e: e.activation(out=ed[i2][0:n, :], in_=pb[pd][0:n, 0:256], func=AF.Exp), r=[("pb", pd)], w=[("ed", i2)], ucls=G)
                P.add("dve", lambda e: e.scalar_tensor_tensor(out=qt_[:, :, t0:t0 + n], in0=pqv[:, :, 0:n], scalar=1.0 / 16.0, in1=eb[i2][:, :, 0:n], op0=ALU.mult, op1=ALU.mult),
                      r=[("pb", pq), ("eb", i2)], w=[("qt", c)], ucls=G)
                P.add("dve", lambda e: e.tensor_tensor(out=kt_[:, :, t0:t0 + n], in0=pkv[:, :, 0:n], in1=emb[i2][:, :, 0:n], op=ALU.mult),
                      r=[("pb", pk), ("emb", i2)], w=[("kt", c)], ucls=G)
                P.add("dve", lambda e: e.tensor_tensor(out=khat[0:n, c, :], in0=pb[pkt][0:n, 0:256], in1=ed[i2][0:n, :], op=ALU.mult),
                      r=[("pb", pkt), ("ed", i2)], w=[("khat", c)], ucls=G)

            def recur(c):
                n = 128 if c < 8 else 16
                t0 = c * 128
                i2 = c % 2
                pA, po, pS0, pS1 = 6, 7, 5, 0
                for kc in range(2):
                    P.add("pe", lambda e, kc=kc: e.matmul(pb[pA][0:n, 0:n], lhsT=kt_[:, kc, t0:t0 + n], rhs=qt_[:, kc, t0:t0 + n], start=(kc == 0), stop=(kc == 1)),
                          r=[("kt", c), ("qt", c)], w=[("pb", pA)], ucls=G)
                P.add("dve", lambda e: e.tensor_tensor(out=AT[i2][0:n, 0:n], in0=pb[pA][0:n, 0:n], in1=mask[0:n, 0:n], op=ALU.mult),
                      r=[("pb", pA), "cst"], w=[("AT", i2)], ucls=G)
                if c < 8:
                    for kc, pS in ((0, pS0), (1, pS1)):
                        P.add("pe", lambda e, kc=kc, pS=pS: e.matmul(pb[pS][:, :], lhsT=khat[0:n, c, kc * 128:(kc + 1) * 128], rhs=v[0:n, c, :], start=True, stop=True),
                              r=[("khat", c), ("v", c)], w=[("pb", pS)], ucls=G)
                for kc in range(2):
                    P.add("pe", lambda e, kc=kc: e.matmul(pb[po][0:n, :], lhsT=qt_[:, kc, t0:t0 + n], rhs=S_bf[:, kc, :], start=(kc == 0), stop=False),
                          r=[("qt", c), "Sbf"], w=[("pb", po)], ucls=G)
                P.add("pe", lambda e: e.matmul(pb[po][0:n, :], lhsT=AT[i2][0:n, 0:n], rhs=v[0:n, c, :], start=False, stop=True),
                      r=[("AT", i2), ("v", c)], w=[("pb", po)], ucls=G)
                if c < 8:
                    for kc, pS in ((0, pS0), (1, pS1)):
                        P.add("dve", lambda e, kc=kc, pS=pS: e.scalar_tensor_tensor(out=S[:, kc, :], in0=S[:, kc, :], scalar=dec[:, kc, c:c + 1], in1=pb[pS][:, :], op0=ALU.mult, op1=ALU.add),
                              r=[("pb", pS), "S", ("dec", c)], w=["S"], ucls=G)
                    P.add("act", lambda e: e.activation(out=S_bf[:], in_=S[:], func=AF.Copy), r=["S"], w=["Sbf"], ucls=G)
                P.add("act", lambda e: e.activation(out=junk[i2][0:n, :], in_=pb[po][0:n, :], func=AF.Square), r=[("pb", po)], w=[("junk", i2)], ucls=G)
                P.add("dve", lambda e: e.reduce_sum(out=ss[0:n, i2:i2 + 1], in_=junk[i2][0:n, :], axis=AX.X), r=[("junk", i2)], w=[("ss", i2)], ucls=G)
                P.add("act", lambda e: e.activation(out=ss[0:n, i2:i2 + 1], in_=ss[0:n, i2:i2 + 1], func=AF.Ln, bias=epsb[0:n, 0:1], scale=1.0 / 512.0), r=[("ss", i2), "eps"], w=[("ss", i2)], ucls=G)
                P.add("act", lambda e: e.activation(out=ss[0:n, i2:i2 + 1], in_=ss[0:n, i2:i2 + 1], func=AF.Exp, scale=-0.5), r=[("ss", i2)], w=[("ss", i2)], ucls=G)
                P.add("dve", lambda e: e.scalar_tensor_tensor(out=v[0:n, c, :], in0=pb[po][0:n, :], scalar=ss[0:n, i2:i2 + 1], in1=hnb[0:n, 0:512], op0=ALU.mult, op1=ALU.mult),
                      r=[("pb", po), ("ss", i2), "sqt"], w=[("v", c)], ucls=G)

            for c in range(NCH):
                inproj(c)
                if c >= 1:
                    recur(c - 1)
            recur(NCH - 1)

            (wr0, kr0), (wr1, kr1) = acquire(2)
            wrv = [wr0[:].rearrange("p (k m) -> p k m", k=KC), wr1[:].rearrange("p (k m) -> p k m", k=KC)]
            krk = [kr0, kr1]
            ptv = pbt[:, 0:512].rearrange("p (k t) -> p k t", k=4)
            def rmm(c):
                n = 128 if c < 8 else 16
                t0 = c * 128
                i2 = c % 2
                pr = 4 + i2
                for hf in range(2):
                    for kc in range(KC):
                        P.add("pe", lambda e, hf=hf, kc=kc: e.matmul(pb[pr][0:n, hf * 256:(hf + 1) * 256], lhsT=h[:, kc, t0:t0 + n], rhs=wrv[hf][:, kc, :], start=(kc == 0), stop=(kc == KC - 1)),
                              r=[krk[hf], ("h", kc)], w=[("pb", pr)])
                P.add("act", lambda e: e.activation(out=junk[i2][0:n, :], in_=pb[pr][0:n, :], func=AF.Silu), r=[("pb", pr)], w=[("junk", i2)], ucls=G)
                P.add("dve", lambda e: e.tensor_tensor(out=v[0:n, c, :], in0=v[0:n, c, :], in1=junk[i2][0:n, :], op=ALU.mult), r=[("v", c), ("junk", i2)], w=[("v", c)], ucls=G)

            def rtr(c):
                n = 128 if c < 8 else 16
                t0 = c * 128
                for vc in range(4):
                    P.add("pe", lambda e, vc=vc: e.transpose(out=ptv[:, vc, 0:n], in_=v[0:n, c, vc * 128:(vc + 1) * 128], identity=ident[0:n, 0:n]),
                          r=[("v", c), "ident"], w=[("pb", 7)], ucls=G)
                P.add("act", lambda e: e.activation(out=ogT[:, :, t0:t0 + n], in_=ptv[:, :, 0:n], func=AF.Copy), r=[("pb", 7)], w=[("ogT", c)], ucls=G)

            for c in range(NCH):
                rmm(c)
                if c >= 1:
                    rtr(c - 1)
            rtr(NCH - 1)
            for dh in range(2):
                (wo, ko), = acquire(1)
                wov = wo[:].rearrange("p (f m) -> p f m", f=4)
                for dd in range(8):
                    d = dh * 8 + dd
                    for ti, (t0, n) in enumerate(TT):
                        b = next_bank()
                        cs_ = [c for c in range(NCH) if c * 128 < t0 + n and (c + 1) * 128 > t0]
                        for vc in range(4):
                            P.add("pe", lambda e, b=b, n=n, t0=t0, vc=vc, dd=dd, wov=wov: e.matmul(pb[b][:, 0:n], lhsT=wov[:, vc, dd * 128:(dd + 1) * 128], rhs=ogT[:, vc, t0:t0 + n], start=(vc == 0), stop=(vc == 3)),
                                  r=[ko] + [("ogT", c) for c in cs_], w=[("pb", b)], ucls=G)
                        P.add("dve", lambda e, b=b, n=n, t0=t0, d=d: e.tensor_tensor(out=x[:, d, t0:t0 + n], in0=x[:, d, t0:t0 + n], in1=pb[b][:, 0:n], op=ALU.add),
                              r=[("pb", b), ("x", d, ti)], w=[("x", d, ti)])
                    if hd == 3:
                        stats_chunk(d)

        for hd in range(4):
            head(hd)

    def gla_prefix():
        cv = Carver()
        khat2 = cv.take([128, 2, 256], BF16)
        v2 = cv.take([128, 2, 512], BF16)
        S = cv.take([128, 2, 512], F32)
        sp_ = [cv.take([128, 256], F32) for _ in range(2)]
        ed = [cv.take([128, 256], F32) for _ in range(2)]
        dec = cv.take([128, 2, 16], F32)
        G = "glap"
        P.add("sp", lambda e: e.dma_start(out=wlra[0:17, 0:1024], in_=wlra_d), w=["rstd"], dsem="c7")
        P.add("dve", lambda e: e.memset(lr_aug[0:32, :], 1.0), w=["sqacc"])
        for ti, (t0, n) in enumerate(TT):
            b = next_bank()
            for kc in range(KC):
                P.add("pe", lambda e, b=b, n=n, t0=t0, kc=kc: e.matmul(pb[b][0:16, 0:n], lhsT=wlr[:, kc * 16:(kc + 1) * 16], rhs=h[:, kc, t0:t0 + n], start=(kc == 0), stop=(kc == KC - 1)),
                      r=["wlr", ("h", kc)], w=[("pb", b)])
            P.add("act", lambda e, b=b, n=n, t0=t0: e.activation(out=lr_aug[0:16, t0:t0 + n], in_=pb[b][0:16, 0:n], func=AF.Copy),
                  r=[("pb", b)], w=["sqacc"])

        def phead(hd):
            (wk, kk), (wv0, kv0), (wv1, kv1) = acquire(3)
            wkv = wk[:].rearrange("p (k m) -> p k m", k=KC)
            wvv = [wv0[:].rearrange("p (k m) -> p k m", k=KC), wv1[:].rearrange("p (k m) -> p k m", k=KC)]
            kvk = [kv0, kv1]
            P.add("dve", lambda e: e.memset(S[:], 0.0), w=["S"], ucls=G)

            def pchunk(c):
                n = 128
                t0 = c * 128
                pz, pbT, pd, pkt, pv = 0, 1, 2, 5, 6
                i2 = c % 2
                P.add("pe", lambda e: e.matmul(pb[pz][0:n, 0:256], lhsT=lr_aug[0:17, t0:t0 + n], rhs=wlra[0:17, hd * 256:(hd + 1) * 256], start=True, stop=True),
                      r=["sqacc", "rstd"], w=[("pb", pz)])
                P.add("act", lambda e: e.activation(out=sp_[i2][0:n, :], in_=pb[pz][0:n, 0:256], func=AF.Exp, scale=-1.0), r=[("pb", pz)], w=[("sp", i2)], ucls=G)
                P.add("act", lambda e: e.activation(out=sp_[i2][0:n, :], in_=sp_[i2][0:n, :], func=AF.Ln, bias=oneb[0:n, 0:1]), r=[("sp", i2), "oneb"], w=[("sp", i2)], ucls=G)
                pbTv = pb[pbT][:, 0:256].rearrange("p (k t) -> p k t", k=2)
                for kc in range(KC):
                    P.add("pe", lambda e, kc=kc: e.matmul(pb[pkt][0:n, 0:256], lhsT=h[:, kc, t0:t0 + n], rhs=wkv[:, kc, :], start=(kc == 0), stop=(kc == KC - 1)),
                          r=[kk, ("h", kc)], w=[("pb", pkt)])
                for hf in range(2):
                    for kc in range(KC):
                        P.add("pe", lambda e, hf=hf, kc=kc: e.matmul(pb[pv][0:n, hf * 256:(hf + 1) * 256], lhsT=h[:, kc, t0:t0 + n], rhs=wvv[hf][:, kc, :], start=(kc == 0), stop=(kc == KC - 1)),
                              r=[kvk[hf], ("h", kc)], w=[("pb", pv)])
                P.add("act", lambda e: e.activation(out=v2[0:n, i2, :], in_=pb[pv][0:n, :], func=AF.Copy), r=[("pb", pv)], w=[("v", i2)], ucls=G)
                for kc in range(2):
                    P.add("pe", lambda e, kc=kc: e.matmul(pbTv[:, kc, 0:n], lhsT=sp_[i2][0:n, kc * 128:(kc + 1) * 128], rhs=uneg[0:n, 0:n], start=True, stop=True),
                          r=[("sp", i2), "cst"], w=[("pb", pbT)], ucls=G)
                P.add("pe", lambda e: e.matmul(pb[pd][0:n, 0:256], lhsT=lneg[0:n, 0:n], rhs=sp_[i2][0:n, :], start=True, stop=True),
                      r=[("sp", i2), "cst"], w=[("pb", pd)], ucls=G)
                P.add("act", lambda e: e.activation(out=dec[:, :, c:c + 1], in_=pbTv[:, :, n - 1:n], func=AF.Exp), r=[("pb", pbT)], w=[("dec", c)], ucls=G)
                P.add("act", lambda e: e.activation(out=ed[i2][0:n, :], in_=pb[pd][0:n, 0:256], func=AF.Exp), r=[("pb", pd)], w=[("ed", i2)], ucls=G)
                P.add("dve", lambda e: e.tensor_tensor(out=khat2[0:n, i2, :], in0=pb[pkt][0:n, 0:256], in1=ed[i2][0:n, :], op=ALU.mult),
                      r=[("pb", pkt), ("ed", i2)], w=[("khat", i2)], ucls=G)

            def pupd(c):
                n = 128
                i2 = c % 2
                for kc, pS in ((0, 3), (1, 4)):
                    P.add("pe", lambda e, kc=kc, pS=pS: e.matmul(pb[pS][:, :], lhsT=khat2[0:n, i2, kc * 128:(kc + 1) * 128], rhs=v2[0:n, i2, :], start=True, stop=True),
                          r=[("khat", i2), ("v", i2)], w=[("pb", pS)], ucls=G)
                    P.add("dve", lambda e, kc=kc, pS=pS: e.scalar_tensor_tensor(out=S[:, kc, :], in0=S[:, kc, :], scalar=dec[:, kc, c:c + 1], in1=pb[pS][:, :], op0=ALU.mult, op1=ALU.add),
                          r=[("pb", pS), "S", ("dec", c)], w=["S"], ucls=G)

            for c in range(8):
                pchunk(c)
                if c >= 1:
                    pupd(c - 1)
            pupd(7)
            P.add("sp", lambda e: e.dma_start(out=sout_d[hd].rearrange("k p v -> p k v"), in_=S[:]), r=["S"], w=[("sout", hd)], dsem="so", ucls=G)

        for hd in range(4):
            phead(hd)

    def pool_mixer(inorm):
        cv = Carver()
        pooled = cv.take([128, KC, T], BF16)
        hp = cv.take([128, T], F32)
        sa = cv.take([128, T], F32)
        sb_ = cv.take([128, T], F32)
        Q = "pool"
        rms_stats(TT, pre_done=True)
        for c in range(KC):
            win = WINS[c // 4]
            P.add("dve", lambda e, c=c: e.scalar_tensor_tensor(out=hp[:], in0=x[:, c, :], scalar=gn[:, inorm * 16 + c:inorm * 16 + c + 1], in1=rstd[:], op0=ALU.mult, op1=ALU.mult),
                  r=XK(c) + ["gn", "rstd"], w=["hp"], ucls=Q)
            src, skey = hp, "hp"
            bufs = [(sa, "sa"), (sb_, "sb")]
            sh = 1
            bi = 0
            while sh < win:
                dst, dkey = bufs[bi % 2]
                bi += 1
                P.add("dve", lambda e, src=src, dst=dst, sh=sh: e.tensor_tensor(out=dst[:, sh:T], in0=src[:, sh:T], in1=src[:, 0:T - sh], op=ALU.add),
                      r=[skey], w=[dkey], ucls=Q)
                P.add("dve", lambda e, src=src, dst=dst, sh=sh: e.tensor_copy(out=dst[:, 0:sh], in_=src[:, 0:sh]), r=[skey], w=[dkey], ucls=Q)
                src, skey = dst, dkey
                sh *= 2
            P.add("dve", lambda e, c=c, src=src, win=win: e.scalar_tensor_tensor(out=pooled[:, c, :], in0=src[:], scalar=1.0 / win, in1=hp[:], op0=ALU.mult, op1=ALU.subtract),
                  r=[skey, "hp"], w=[("pooled", c)], ucls=Q)
        for s in range(2):
            (wp, kp), = acquire(1)
            wpv = wp[:].rearrange("p (g k m) -> p g k m", g=2, k=4)
            for gi in range(2):
                g = 2 * s + gi
                for dc in range(4):
                    d = g * 4 + dc
                    for ti, (t0, n) in enumerate(TT):
                        b = next_bank()
                        for kc in range(4):
                            P.add("pe", lambda e, b=b, n=n, t0=t0, gi=gi, kc=kc, dc=dc, g=g, wpv=wpv: e.matmul(pb[b][:, 0:n], lhsT=wpv[:, gi, kc, dc * 128:(dc + 1) * 128], rhs=pooled[:, g * 4 + kc, t0:t0 + n], start=(kc == 0), stop=(kc == 3)),
                                  r=[kp, ("pooled", g * 4 + kc)], w=[("pb", b)], ucls=Q)
                        P.add("dve", lambda e, b=b, n=n, d=d: e.tensor_scalar(out=sa[:, 0:n], in0=pb[b][:, 0:n], scalar1=pbs[:, d:d + 1], scalar2=pbs[:, 16 + d:16 + d + 1], op0=ALU.add, op1=ALU.mult),
                              r=[("pb", b), "pbs"], w=["sa"], ucls=Q)
                        P.add("dve", lambda e, n=n, t0=t0, d=d: e.tensor_tensor(out=x[:, d, t0:t0 + n], in0=x[:, d, t0:t0 + n], in1=sa[:, 0:n], op=ALU.add),
                              r=["sa", ("x", d, ti)], w=[("x", d, ti)], ucls=Q)
                    stats_chunk(d)

    def final():
        rms_stats(TTM, pre_done=True)
        for c in range(KC):
            eng = "dve"
            P.add(eng, lambda e, c=c: e.scalar_tensor_tensor(out=x[:, c, 16:T], in0=x[:, c, 16:T], scalar=gn[:, 6 * 16 + c:6 * 16 + c + 1], in1=rstd[:, 16:T], op0=ALU.mult, op1=ALU.mult),
                  r=XK(c) + ["gn", "rstd"], w=XK(c))
        dump()

    def dump():
        for c4 in range(4):
            P.add("sp", lambda e, c4=c4: e.dma_start(out=yout[512 * c4:512 * c4 + 512, :].rearrange("(c p) t -> p c t", p=128), in_=x[:, 4 * c4:4 * c4 + 4, 16:T]),
                  r=[k for i in range(4) for k in XK(4 * c4 + i)], w=[("yout", c4)], dsem="y%d" % c4)
        P.add("sp", None, r=[("yout", c4) for c4 in range(4)] + [("sout", hd) for hd in range(4)])

    TA = TT[:2]
    rmsnorm(0, TA)
    ffn(TA, stats_chunk)
    rmsnorm(1, TA, pre_done=True)
    gla_prefix()
    load_x(xin)
    stages = [
        ("n0", lambda: rmsnorm(0)), ("f0", lambda: ffn(TT, stats_chunk)),
        ("n1", lambda: rmsnorm(1, TT, True)), ("gla", gla),
        ("n2", lambda: rmsnorm(2, TT, True)), ("f1", lambda: ffn(TT, stats_chunk)),
        ("n3", lambda: rmsnorm(3, TT, True)), ("f2", lambda: ffn(TT, stats_chunk)),
        ("pool", lambda: pool_mixer(4)),
        ("n5", lambda: rmsnorm(5, TTM, True)), ("f3", lambda: ffn(TTM, stats_chunk)),
    ]
    done = False
    for name, fn in stages:
        fn()
        if stop_after == name:
            dump()
            done = True
            break
    if not done:
        final()
    P.emit()
    P.close()
    return nc


_NC_CACHE = {}


def _consts():
    j = np.arange(128)[:, None]
    i = np.arange(128)[None, :]
    ones = np.full((128, 128), 1.0 / D, np.float32)
    uneg = np.where(j <= i, -1.0 / 16.0, 0.0).astype(np.float32)
    lneg = np.where(j > i, -1.0 / 16.0, 0.0).astype(np.float32)
    mask = np.where(j <= i, 1.0, 0.0).astype(np.float32)
    ident = np.eye(128, dtype=np.float32)
    return np.ascontiguousarray(np.concatenate([ones, uneg, lneg, mask, ident], axis=1))


def prepare(x, meta, ffn_norm, ffn_w_gate, ffn_w_up, ffn_w_down, gla_norm, gla_w_in, gla_w_lr, gla_b_lr,
            gla_head_norm, gla_w_out, pool_norm, pool_w, pool_b, pool_scale, final_norm):
    f = lambda a: np.asarray(a, dtype=np.float32)
    x = f(x)
    meta = f(meta)
    slabs = []
    slabs += ffn_slabs(f(ffn_w_gate[0, 0]), f(ffn_w_up[0, 0]), f(ffn_w_down[0, 0]))
    slabs += gla_slabs(f(gla_w_in[0]), f(gla_w_out[0]))
    slabs += ffn_slabs(f(ffn_w_gate[0, 1]), f(ffn_w_up[0, 1]), f(ffn_w_down[0, 1]))
    slabs += ffn_slabs(f(ffn_w_gate[1, 0]), f(ffn_w_up[1, 0]), f(ffn_w_down[1, 0]))
    slabs += pool_slabs(f(pool_w[0]))
    slabs += ffn_slabs(f(ffn_w_gate[1, 1]), f(ffn_w_up[1, 1]), f(ffn_w_down[1, 1]))
    WS = np.ascontiguousarray(np.stack(slabs, axis=0))
    assert WS.shape[0] == N_SLABS
    gains = np.stack([f(ffn_norm[0, 0]), f(gla_norm[0]), f(ffn_norm[0, 1]), f(ffn_norm[1, 0]), f(pool_norm[0]), f(ffn_norm[1, 1]), f(final_norm)], axis=0)
    gn = np.ascontiguousarray(gains.reshape(7, KC, 128).transpose(2, 0, 1).reshape(128, 7 * 16))
    wlr = np.ascontiguousarray(f(gla_w_in[0])[:, 4096:4112].reshape(KC, 128, 16).transpose(1, 0, 2).reshape(128, 256))
    wlra = np.ascontiguousarray(np.concatenate([f(gla_w_lr[0]), f(gla_b_lr[0])[None, :]], axis=0))
    hnb = np.ascontiguousarray(np.broadcast_to(f(gla_head_norm[0])[None, :], (128, 512)))
    pbv = f(pool_b[0]).reshape(D).reshape(KC, 128).T
    psv = f(pool_scale[0]).reshape(KC, 128).T
    pbs = np.ascontiguousarray(np.concatenate([pbv, psv], axis=1))
    common = dict(WS=WS, gn=gn, cst=_consts(), wlr=wlr, wlra=wlra, hnb=hnb, pbs=pbs)
    xins = []
    xpres = []
    zpre = np.zeros((D, T), np.float32)
    for b in range(4):
        seq0 = np.concatenate([meta, x[b, 0:1024]], axis=0)
        seq1 = x[b, 1008:2048]
        xins.append(np.ascontiguousarray(seq0.T))
        xins.append(np.ascontiguousarray(seq1.T))
        pre1 = np.zeros((D, T), np.float32)
        pre1[:, 0:1024] = np.concatenate([meta, x[b, 0:1008]], axis=0).T
        xpres.append(zpre)
        xpres.append(pre1)
    return common, xins, xpres


def run(inputs, stop_after=None, trace=False):
    common, xins, xpres = prepare(**inputs)
    if stop_after not in _NC_CACHE:
        _NC_CACHE[stop_after] = build(stop_after)
    nc = _NC_CACHE[stop_after]
    in_maps = [dict(common, xin=xins[c], xpre=xpres[c]) for c in range(8)]
    res = run_bass_kernel_spmd(nc, in_maps, core_ids=list(range(8)), trace=trace)
    out = np.empty((4, 2048, D), np.float32)
    for c in range(8):
        b, hf = c // 2, c % 2
        out[b, hf * 1024:(hf + 1) * 1024, :] = res.results[c]["yout"].T
    return out, (res, res)


def kernel(**inputs):
    out, _ = run(inputs)
    return out
```
